# Optimizing a Trainium2 kernel written in Bass

```python
import jax, jax.numpy as jnp
from jax import lax
import numpy as np

D_MODEL = 1024
BATCH = 1
SEQ = 16384
DEPTH = 1
DEC_BATCH = 128
DEC_SEQ = 1
PAST_LEN = 16384
PAGE_SIZE = 128

HEAD_DIM = 64
ATTN_HEADS = 8
KV_HEADS = 2
GQA_GROUP = ATTN_HEADS // KV_HEADS
WINDOW = 128
ROPE_THETA = 10000.0
GM_HEADS = 8
GM_HEAD_DIM = 64
CHUNK = 128
ATTN_WIDTH = ATTN_HEADS * HEAD_DIM
KV_WIDTH = KV_HEADS * HEAD_DIM
GM_WIDTH = GM_HEADS * GM_HEAD_DIM
D_MIX = ATTN_WIDTH + GM_WIDTH
D_IN = ATTN_WIDTH + 2 * KV_WIDTH + 2 * GM_WIDTH
SPLITS = (ATTN_WIDTH, ATTN_WIDTH + KV_WIDTH, ATTN_WIDTH + 2 * KV_WIDTH, ATTN_WIDTH + 2 * KV_WIDTH + GM_WIDTH)
PEER_HEADS = 8
N_KEYS = 128
N_EXPERTS = N_KEYS * N_KEYS
D_KEY = 256
HALF_KEY = D_KEY // 2
TOPK = 16
PEER_BLOCK = 128
EPS = 1e-6
NEG_INF = -1e30

kernel_name = "hymba_swa_sink_gmlp_peer_step"


def rms_norm(x, w):
    xf = x.astype(jnp.float32)
    y = xf * lax.rsqrt(jnp.mean(xf * xf, axis=-1, keepdims=True) + EPS)
    return (y * w.astype(jnp.float32)).astype(x.dtype)


def rope(x, pos):
    half = HEAD_DIM // 2
    inv = ROPE_THETA ** (-jnp.arange(half, dtype=jnp.float32) / half)
    ang = pos.astype(jnp.float32)[:, None] * inv[None, :]
    cos = jnp.cos(ang)[:, None, :]
    sin = jnp.sin(ang)[:, None, :]
    xf = x.astype(jnp.float32)
    x1, x2 = xf[..., :half], xf[..., half:]
    return jnp.concatenate([x1 * cos - x2 * sin, x2 * cos + x1 * sin], axis=-1).astype(x.dtype)


def mixer_inputs(xn, pos, w_in, q_norm_w, k_norm_w, gm_v_norm_w):
    B, S = xn.shape[:2]
    q, k, v, u, gv = jnp.split(xn @ w_in, SPLITS, axis=-1)
    q = rope(rms_norm(q.reshape(B, S, ATTN_HEADS, HEAD_DIM), q_norm_w), pos)
    k = rope(rms_norm(k.reshape(B, S, KV_HEADS, HEAD_DIM), k_norm_w), pos)
    v = v.reshape(B, S, KV_HEADS, HEAD_DIM)
    u = jax.nn.gelu(u, approximate=False)
    gv = rms_norm(jax.nn.gelu(gv, approximate=False).reshape(B, S, GM_HEADS, GM_HEAD_DIM), gm_v_norm_w)
    return q, k, v, u, gv


def sink_attention(q, k, v, sinks, mask):
    B, N, Q = q.shape[:3]
    qg = q.reshape(B, N, Q, KV_HEADS, GQA_GROUP, HEAD_DIM).astype(jnp.float32)
    s = jnp.einsum('bnqkgd,bnlkd->bnkgql', qg, k.astype(jnp.float32)) * (HEAD_DIM ** -0.5)
    s = jnp.where(mask[None, :, None, None], s, NEG_INF)
    sink = sinks.astype(jnp.float32).reshape(KV_HEADS, GQA_GROUP)[None, None, :, :, None, None]
    m = jnp.maximum(jnp.max(s, axis=-1, keepdims=True), sink)
    p = jnp.exp(s - m)
    denom = jnp.sum(p, axis=-1, keepdims=True) + jnp.exp(sink - m)
    o = jnp.einsum('bnkgql,bnlkd->bnqkgd', p / denom, v.astype(jnp.float32))
    return o.reshape(B, N, Q, ATTN_WIDTH).astype(q.dtype)


def spatial_gate(u, gv_chunks, w_spatial, b_spatial):
    C = gv_chunks.shape[2]
    w = jnp.tril(w_spatial[:, :C, :C])
    mixed = jnp.einsum('hts,bnshd->bnthd', w, gv_chunks) + b_spatial[:, :C].T[:, :, None]
    return u * mixed.reshape(u.shape)


def merge_heads(attn, gm, out_norm_w, w_out):
    a = rms_norm(attn, out_norm_w[:ATTN_WIDTH])
    g = rms_norm(gm, out_norm_w[ATTN_WIDTH:])
    return jnp.concatenate([a, g], axis=-1) @ w_out


def peer_tokens(xn, w_query, sub_keys, expert_u, expert_v):
    n = xn.shape[0]
    q = (xn @ w_query).reshape(n, PEER_HEADS, 2, HALF_KEY).astype(jnp.float32)
    s = jnp.einsum('nhpc,hpkc->nhpk', q, sub_keys.astype(jnp.float32))
    s_top, i_top = lax.top_k(s, TOPK)
    cand = (s_top[:, :, 0, :, None] + s_top[:, :, 1, None, :]).reshape(n, PEER_HEADS, TOPK * TOPK)
    best, idx = lax.top_k(cand, TOPK)
    i1 = jnp.take_along_axis(i_top[:, :, 0], idx // TOPK, axis=-1)
    i2 = jnp.take_along_axis(i_top[:, :, 1], idx % TOPK, axis=-1)
    expert = i1 * N_KEYS + i2
    g = jax.nn.softmax(best, axis=-1)
    act = jax.nn.gelu(jnp.einsum('nhkd,nd->nhk', expert_u[expert], xn).astype(jnp.float32), approximate=False)
    return jnp.einsum('nhk,nhkd->nd', (g * act).astype(xn.dtype), expert_v[expert])


def peer(xn, w_query, sub_keys, expert_u, expert_v):
    shp = xn.shape
    flat = xn.reshape(-1, D_MODEL)
    n = flat.shape[0]
    nb = -(-n // PEER_BLOCK)
    flat = jnp.pad(flat, ((0, nb * PEER_BLOCK - n), (0, 0)))
    out = lax.map(lambda blk: peer_tokens(blk, w_query, sub_keys, expert_u, expert_v),
                  flat.reshape(nb, PEER_BLOCK, D_MODEL))
    return out.reshape(-1, D_MODEL)[:n].reshape(shp)


def setup_inputs(seed: int = 0) -> dict:
    key = jax.random.key(seed)
    ks = jax.random.split(key, 20)
    f32 = jnp.float32
    nrm = lambda k, shape, scale: (scale * jax.random.normal(k, shape)).astype(f32)
    gain = lambda k, shape: (1.0 + 0.01 * jax.random.normal(k, shape)).astype(f32)
    return {
        "x_prompt": nrm(ks[0], (BATCH, SEQ, D_MODEL), 1.0),
        "x_sample": nrm(ks[1], (DEC_BATCH, DEC_SEQ, D_MODEL), 1.0),
        "cache_k": nrm(ks[2], (DEPTH, DEC_BATCH, WINDOW, KV_HEADS, HEAD_DIM), 1.0),
        "cache_v": nrm(ks[3], (DEPTH, DEC_BATCH, WINDOW, KV_HEADS, HEAD_DIM), 1.0),
        "norm_mix_w": gain(ks[4], (DEPTH, D_MODEL)),
        "w_in": nrm(ks[5], (DEPTH, D_MODEL, D_IN), D_MODEL ** -0.5),
        "q_norm_w": gain(ks[6], (DEPTH, HEAD_DIM)),
        "k_norm_w": gain(ks[7], (DEPTH, HEAD_DIM)),
        "sinks": nrm(ks[8], (DEPTH, ATTN_HEADS), 0.5),
        "gm_v_norm_w": gain(ks[9], (DEPTH, GM_HEADS, GM_HEAD_DIM)),
        "w_spatial": nrm(ks[10], (DEPTH, GM_HEADS, CHUNK, CHUNK), CHUNK ** -0.5),
        "b_spatial": (1.0 + 0.1 * jax.random.normal(ks[11], (DEPTH, GM_HEADS, CHUNK))).astype(f32),
        "out_norm_w": gain(ks[12], (DEPTH, D_MIX)),
        "w_out": nrm(ks[13], (DEPTH, D_MIX, D_MODEL), D_MIX ** -0.5),
        "norm_ffn_w": gain(ks[14], (DEPTH, D_MODEL)),
        "w_query": nrm(ks[15], (DEPTH, D_MODEL, PEER_HEADS * D_KEY), D_MODEL ** -0.5),
        "sub_keys": nrm(ks[16], (DEPTH, PEER_HEADS, 2, N_KEYS, HALF_KEY), HALF_KEY ** -0.5),
        "expert_u": nrm(ks[17], (DEPTH, N_EXPERTS, D_MODEL), D_MODEL ** -0.5),
        "expert_v": nrm(ks[18], (DEPTH, N_EXPERTS, D_MODEL), D_MODEL ** -0.5),
    }


def reference(x_prompt, x_sample, cache_k, cache_v, norm_mix_w, w_in, q_norm_w, k_norm_w, sinks,
              gm_v_norm_w, w_spatial, b_spatial, out_norm_w, w_out, norm_ffn_w, w_query, sub_keys,
              expert_u, expert_v):
    pos_p = jnp.arange(SEQ, dtype=jnp.int32)
    pos_s = PAST_LEN + jnp.arange(DEC_SEQ, dtype=jnp.int32)
    nb = SEQ // WINDOW
    qi = jnp.arange(WINDOW)[:, None]
    kj = jnp.arange(2 * WINDOW)[None, :]
    diff = qi + WINDOW - kj
    band = (diff >= 0) & (diff < WINDOW)
    mask_p = band[None] & ((jnp.arange(nb)[:, None, None] > 0) | (kj[None] >= WINDOW))
    si = jnp.arange(DEC_SEQ)[:, None]
    sj = jnp.arange(WINDOW + DEC_SEQ)[None, :]
    sdiff = si + WINDOW - sj
    mask_s = ((sdiff >= 0) & (sdiff < WINDOW))[None]

    h_p, h_s = x_prompt, x_sample
    nk_p, nv_p, nk_s, nv_s, ngv_s = [], [], [], [], []
    for layer in range(DEPTH):
        xn = rms_norm(h_p, norm_mix_w[layer])
        q, k, v, u, gv = mixer_inputs(xn, pos_p, w_in[layer], q_norm_w[layer], k_norm_w[layer], gm_v_norm_w[layer])
        kb = k.reshape(BATCH, nb, WINDOW, KV_HEADS, HEAD_DIM)
        vb = v.reshape(BATCH, nb, WINDOW, KV_HEADS, HEAD_DIM)
        pad_blk = ((0, 0), (1, 0), (0, 0), (0, 0), (0, 0))
        k_band = jnp.concatenate([jnp.pad(kb[:, :-1], pad_blk), kb], axis=2)
        v_band = jnp.concatenate([jnp.pad(vb[:, :-1], pad_blk), vb], axis=2)
        attn = sink_attention(q.reshape(BATCH, nb, WINDOW, ATTN_HEADS, HEAD_DIM), k_band, v_band,
                              sinks[layer], mask_p).reshape(BATCH, SEQ, ATTN_WIDTH)
        gm = spatial_gate(u, gv.reshape(BATCH, SEQ // CHUNK, CHUNK, GM_HEADS, GM_HEAD_DIM),
                          w_spatial[layer], b_spatial[layer])
        h_p = h_p + merge_heads(attn, gm, out_norm_w[layer], w_out[layer])
        h_p = h_p + peer(rms_norm(h_p, norm_ffn_w[layer]), w_query[layer], sub_keys[layer], expert_u[layer], expert_v[layer])
        nk_p.append(k[:, -WINDOW:])
        nv_p.append(v[:, -WINDOW:])
        xn = rms_norm(h_s, norm_mix_w[layer])
        q, k, v, u, gv = mixer_inputs(xn, pos_s, w_in[layer], q_norm_w[layer], k_norm_w[layer], gm_v_norm_w[layer])
        k_all = jnp.concatenate([cache_k[layer].astype(k.dtype), k], axis=1)
        v_all = jnp.concatenate([cache_v[layer].astype(v.dtype), v], axis=1)
        attn = sink_attention(q[:, None], k_all[:, None], v_all[:, None], sinks[layer],
                              mask_s).reshape(DEC_BATCH, DEC_SEQ, ATTN_WIDTH)
        gm = spatial_gate(u, gv[:, None], w_spatial[layer], b_spatial[layer])
        h_s = h_s + merge_heads(attn, gm, out_norm_w[layer], w_out[layer])
        h_s = h_s + peer(rms_norm(h_s, norm_ffn_w[layer]), w_query[layer], sub_keys[layer], expert_u[layer], expert_v[layer])
        nk_s.append(k_all[:, -WINDOW:])
        nv_s.append(v_all[:, -WINDOW:])
        ngv_s.append(gv)
    return (h_p, h_s, jnp.stack(nk_p), jnp.stack(nv_p), jnp.stack(nk_s), jnp.stack(nv_s), jnp.stack(ngv_s))
```

```python
import contextlib
import threading
import numpy as np
import concourse.bass as bass
import concourse.mybir as mybir
from concourse.bass_utils import run_bass_kernel_spmd

F32 = mybir.dt.float32
BF16 = mybir.dt.bfloat16
U32 = mybir.dt.uint32
I32 = mybir.dt.int32
AF = mybir.ActivationFunctionType
ALU = mybir.AluOpType
AX = mybir.AxisListType

NCORES = 8
TPC = 2048
NT = 16
NS = 16
NTOK = TPC + NS
EPS = 1e-6
NCH = 128
TT = 384

ENGS = ("pe", "act", "dve", "pool", "sp")


class Sched:
    EXTCNT = {}

    def __init__(self, nc, tag, ext=None):
        self.nc = nc
        self.tag = tag
        self.ext = ext or {}
        self.streams = {e: [] for e in ENGS}
        self.cnt = {e: 0 for e in ENGS}
        self.waited = {e: {} for e in ENGS}
        self.last_w = {}
        self.readers = {}
        self.dma_cnt = {}
        self.sem_keys = list(ENGS)
        self.final_waits = {}

    def _deps(self, eng, reads, writes):
        deps = []
        for r in reads:
            t = self.last_w.get(r)
            if t is not None:
                deps.append(t)
        for w in writes:
            t = self.last_w.get(w)
            if t is not None:
                deps.append(t)
            deps.extend(self.readers.get(w, ()))
        wd = self.waited[eng]
        best = {}
        for (k, v) in deps:
            if eng == "pe" and k == "pe":
                continue
            if wd.get(k, 0) >= v:
                continue
            if best.get(k, 0) < v:
                best[k] = v
        waits = []
        for k, v in best.items():
            wd[k] = v
            waits.append((k, v))
        return waits

    def _commit(self, tok, reads, writes):
        for r in reads:
            self.readers.setdefault(r, []).append(tok)
        for w in writes:
            self.last_w[w] = tok
            self.readers[w] = []

    tls = threading.local()

    @staticmethod
    def _hook():
        h = getattr(Sched.tls, "hook", None)
        if h is not None and not getattr(Sched.tls, "open", None):
            h()

    @staticmethod
    def _track_banks(eng, writes):
        op = getattr(Sched.tls, "open", None)
        if op is None:
            return
        for w in writes:
            if len(w) == 2 and w[0] in "POT" and w[1].isdigit():
                if eng == "pe":
                    op.add(w)
                else:
                    op.discard(w)

    def op(self, eng, fn, reads=(), writes=()):
        Sched._hook()
        Sched._track_banks(eng, writes)
        waits = self._deps(eng, reads, writes)
        self.cnt[eng] += 1
        tok = (eng, self.cnt[eng])
        self.streams[eng].append((waits, fn, (eng, 1)))
        self._commit(tok, reads, writes)

    def dma(self, queue, fn, semkey, reads=(), writes=(), final=False, extra_waits=()):
        Sched._hook()
        if semkey in self.ext:
            k = semkey
            Sched.EXTCNT[k] = Sched.EXTCNT.get(k, 0) + 1
            waits = self._deps(queue, reads, writes)
            self.streams[queue].append((waits, fn, (k, 16)))
            return
        k = "d:" + semkey
        if k not in self.dma_cnt:
            self.dma_cnt[k] = 0
            self.sem_keys.append(k)
        waits = self._deps(queue, reads, writes) + list(extra_waits)
        self.dma_cnt[k] += 1
        tok = (k, 16 * self.dma_cnt[k])
        self.streams[queue].append((waits, fn, (k, 16)))
        self._commit(tok, reads, writes)
        if final:
            self.final_waits[k] = tok[1]

    def emit(self, final_engine="sp"):
        nc = self.nc
        with contextlib.ExitStack() as es:
            sems = dict(self.ext)
            for i, k in enumerate(self.sem_keys):
                sems[k] = es.enter_context(nc.semaphore(f"{self.tag}_{i}"))
            block = es.enter_context(nc.Block())
            engmap = {"pe": block.tensor, "act": block.scalar, "dve": block.vector,
                      "pool": block.gpsimd, "sp": block.sync}
            for e in ENGS:
                stream = self.streams[e]
                fw = self.final_waits if e == final_engine else {}

                def body(eng, stream=stream, fw=fw):
                    for (waits, fn, (ik, iv)) in stream:
                        for (k, v) in waits:
                            eng.wait_ge(sems[k], v)
                        fn(eng).then_inc(sems[ik], iv)
                    for k, v in fw.items():
                        eng.wait_ge(sems[k], v)

                if stream or fw:
                    engmap[e](body)


import os
STAGE = int(os.environ.get('KSTAGE', '9'))
NTL = int(os.environ.get('KNT', '16'))
KSAMP = int(os.environ.get('KSAMP', '1'))
KSUB = int(os.environ.get('KSUB', '9'))


def interleave(fa, fb):
    turn = [threading.Semaphore(0), threading.Semaphore(0)]
    alive = [True, True]
    err = []

    def runner(i, f):
        turn[i].acquire()

        def h():
            j = 1 - i
            if alive[j]:
                turn[j].release()
                turn[i].acquire()
        Sched.tls.hook = h
        Sched.tls.open = set()
        try:
            f()
        except BaseException as ex:
            err.append(ex)
        finally:
            alive[i] = False
            Sched.tls.hook = None
            Sched.tls.open = None
            turn[1 - i].release()

    ths = [threading.Thread(target=runner, args=(i, f)) for i, f in enumerate((fa, fb))]
    for t in ths:
        t.start()
    turn[0].release()
    for t in ths:
        t.join()
    if err:
        raise err[0]


def build_nc(phases=3, dbg=False):
    nc = bass.Bass("TRN2", target_bir_lowering=False)

    def din(name, shape, dt=F32):
        return nc.dram_tensor(name, list(shape), dt, kind="ExternalInput").ap()

    def dout(name, shape, dt=F32):
        return nc.dram_tensor(name, list(shape), dt, kind="ExternalOutput").ap()

    xp = din("xp", [TPC + 128, 1024])
    xs = din("xs", [NS, 1024])
    ck = din("ck", [NS, 128, 128])
    cv = din("cv", [NS, 128, 128])
    rtp = din("rtp", [TPC + 128, 96])
    rts = din("rts", [NS, 96])
    masks = din("masks", [3, 128, 128])
    ident_d = din("ident", [128, 128])
    iota_d = din("iota", [128, 128])
    iotai_d = din("iotai", [128, 128], I32)
    tril_d = din("tril", [128, 128])
    nwmix_d = din("nwmix", [128, 8])
    nwout_d = din("nwout", [128, 8])
    nwffn_d = din("nwffn", [128, 8])
    nwqk_d = din("nwqk", [128, 640])
    nwgv_d = din("nwgv", [128, 512])
    sinks_d = din("sinksr", [128, 8])
    ws0_d = din("ws0", [128, 8])
    b0_d = din("b0", [128, 8])
    bT_d = din("bT", [128, 8])
    win_d = din("win", [128, 8, 1792])
    wout_d = din("wout", [128, 8, 1024])
    wq_d = din("wq", [128, 8, 2048])
    skT_d = din("skT", [128, 16, 128])
    wsT_d = din("wsT", [128, 8, 128])
    UT_d = din("UT", [NCH, 128, 1024])
    V_d = din("Vv", [NCH, 128, 1024])

    y_p = dout("y_p", [TPC, 1024])
    y_s = dout("y_s", [NS, 1024])
    nk_p = dout("nk_p", [128, 128])
    nv_p = dout("nv_p", [128, 128])
    nk_s = dout("nk_s", [NS, 128, 128])
    nv_s = dout("nv_s", [NS, 128, 128])
    ngv_s = dout("ngv_s", [NS, 512])

    UTs = nc.dram_tensor("UTs", [NCH, 128, 1024], BF16, kind="Internal").ap()
    Vs = nc.dram_tensor("Vs", [NCH, 128, 1024], BF16, kind="Internal").ap()

    def ytok(t0, n):
        if t0 >= TPC:
            return y_s[t0 - TPC:t0 - TPC + n, :]
        return y_p[t0:t0 + n, :]

    with contextlib.ExitStack() as top:
        def sbt(es, name, shape, dt):
            return es.enter_context(nc.sbuf_tensor("s_" + name, list(shape), dt))

        def pst(es, name, shape, dt):
            return es.enter_context(nc.psum_tensor("p_" + name, list(shape), dt))

        XT = sbt(top, "XT", [128, 8, NTOK], BF16)
        I1T = sbt(top, "I1T", [128, NTOK], BF16)
        I2T = sbt(top, "I2T", [128, NTOK], BF16)
        GT = sbt(top, "GT", [128, NTOK], BF16)
        identf = sbt(top, "identf", [128, 128], F32)
        identb = sbt(top, "identb", [128, 128], BF16)
        iotaf = sbt(top, "iotaf", [128, 128], F32)
        iotab = sbt(top, "iotab", [128, 128], BF16)
        Sched.EXTCNT.clear()
        EXT = {}
        for qq in range(4):
            EXT[f"x:UTs{qq}"] = top.enter_context(nc.semaphore(f"xUTs{qq}"))
            EXT[f"x:Vs{qq}"] = top.enter_context(nc.semaphore(f"xVs{qq}"))

        with contextlib.ExitStack() as es:
            S = Sched(nc, "a")
            dv = lambda fn, r=(), w=(): S.op("dve", fn, r, w)
            ac = lambda fn, r=(), w=(): S.op("act", fn, r, w)
            pe = lambda fn, r=(), w=(): S.op("pe", fn, r, w)
            pl = lambda fn, r=(), w=(): S.op("pool", fn, r, w)

            win_b = sbt(es, "win_b", [128, 8, 1792], BF16)
            wout_b = sbt(es, "wout_b", [128, 8, 1024], BF16)
            wsT_f = sbt(es, "wsT_f", [128, 8, 128], F32)
            wsT_b = sbt(es, "wsT_b", [128, 8, 128], BF16)
            trilf = sbt(es, "trilf", [128, 128], F32)
            mk_f = sbt(es, "mk_f", [128, 3, 128], F32)
            mk_b = sbt(es, "mk_b", [128, 3, 128], BF16)
            nwmix = sbt(es, "nwmix", [128, 8], F32)
            nwout = sbt(es, "nwout", [128, 8], F32)
            nwffn = sbt(es, "nwffn", [128, 8], F32)
            nwqk = sbt(es, "nwqk", [128, 640], F32)
            nwgv = sbt(es, "nwgv", [128, 512], F32)
            esink = sbt(es, "esink", [128, 8], F32)
            ws0 = sbt(es, "ws0", [128, 8], F32)
            b0 = sbt(es, "b0", [128, 8], F32)
            bT = sbt(es, "bT", [128, 8], F32)
            epst = sbt(es, "epst", [128, 1], F32)

            xt = [sbt(es, f"xt{i}", [128, 1024], F32) for i in range(2)]
            rt = [sbt(es, f"rt{i}", [128, 96], F32) for i in range(2)]
            junk = sbt(es, "junk", [128, 1024], BF16)
            st = sbt(es, "st", [128, 64], F32)
            xn_b = sbt(es, "xn_b", [128, 1024], BF16)
            xnT = sbt(es, "xnT", [128, 8, 128], BF16)
            qk = sbt(es, "qk", [128, 640], F32)
            qkt = sbt(es, "qkt", [128, 640], F32)
            qkn = sbt(es, "qkn", [128, 640], F32)
            rB = sbt(es, "rB", [128, 2, 10, 32], F32)
            qs = sbt(es, "qs", [128, 512], BF16)
            ks = sbt(es, "ks", [128, 128], BF16)
            kf = sbt(es, "kf", [128, 128], F32)
            vf = sbt(es, "vf", [128, 128], F32)
            vaug = [sbt(es, f"vaug{i}", [128, 2, 65], BF16) for i in range(3)]
            KT = [sbt(es, f"KT{i}", [128, 128], BF16) for i in range(3)]
            QT2 = [sbt(es, f"QT{i}", [128, 512], BF16) for i in range(2)]
            PT = sbt(es, "PT", [128, 4, 512], BF16)
            den = sbt(es, "den", [128, 8], F32)
            attn = sbt(es, "attn", [128, 512], F32)
            ug2 = [sbt(es, f"ug{i}", [128, 512], F32) for i in range(2)]
            gvg2 = [sbt(es, f"gvg{i}", [128, 512], F32) for i in range(2)]
            junkB = sbt(es, "junkB", [128, 1024], BF16)
            xn2_b = sbt(es, "xn2_b", [128, 1024], BF16)
            gvt = sbt(es, "gvt", [128, 512], F32)
            gvn_f = sbt(es, "gvn_f", [128, 512], F32)
            gvn_b = sbt(es, "gvn_b", [128, 512], BF16)
            gm = sbt(es, "gm", [128, 512], F32)
            cat_b = sbt(es, "cat_b", [128, 1024], BF16)
            catT = sbt(es, "catT", [128, 8, 128], BF16)
            ht = [sbt(es, f"ht{i}", [128, 1024], F32) for i in range(2)]
            KSj = sbt(es, "KSj", [128, NS, 128], F32)
            VSj = sbt(es, "VSj", [128, NS, 128], F32)
            KSb = sbt(es, "KSb", [128, NS, 128], BF16)
            KTs = sbt(es, "KTs", [128, NS, 128], BF16)
            VSa = sbt(es, "VSa", [128, NS, 2, 65], BF16)
            PTs = sbt(es, "PTs", [128, 128], BF16)
            OsT = sbt(es, "OsT", [65, 128], F32)

            P = [pst(es, f"P{i}", [128, 512], F32) for i in range(4)]
            O = [pst(es, f"O{i}", [128, 512], F32) for i in range(2)]
            T = [pst(es, f"T{i}", [128, 1024], BF16) for i in range(2)]

            def ld(dst, src, name, q="sp"):
                S.dma(q, lambda e: e.dma_start(out=dst, in_=src), name, writes=[name])

            ld(identf[:], ident_d, "identf")
            ld(iotaf[:], iota_d, "iotaf")
            ld(trilf[:], tril_d, "trilf")
            ld(mk_f[:], masks.rearrange("m j i -> j m i"), "mk_f")
            ld(nwmix[:], nwmix_d, "nwmix")
            ld(nwout[:], nwout_d, "nwout")
            ld(nwffn[:], nwffn_d, "nwffn")
            ld(nwqk[:], nwqk_d, "nwqk")
            ld(nwgv[:], nwgv_d, "nwgv")
            ld(esink[:], sinks_d, "esink")
            ld(ws0[:], ws0_d, "ws0")
            ld(b0[:], b0_d, "b0")
            ld(bT[:], bT_d, "bT")
            ld(wsT_f[:], wsT_d, "wsT_f")
            for nt_i, (n0, nw) in ((1, (512, 256)), (0, (0, 512)), (2, (768, 512)), (3, (1280, 512))):
                S.dma("pool", lambda e, n0=n0, nw=nw: e.dma_start(out=win_b[:, :, n0:n0 + nw], in_=win_d[:, :, n0:n0 + nw]), f"win_b{nt_i}", writes=[f"win_b{nt_i}"])
            for dc in range(8):
                S.dma("pool", lambda e, dc=dc: e.dma_start(out=wout_b[:, dc, :], in_=wout_d[:, dc, :]), "wout_b", writes=["wout_b"])
            dv(lambda e: e.tensor_copy(out=identb[:], in_=identf[:]), ["identf"], ["identb"])
            dv(lambda e: e.tensor_copy(out=iotab[:], in_=iotaf[:]), ["iotaf"], ["iotab"])
            dv(lambda e: e.tensor_copy(out=mk_b[:], in_=mk_f[:]), ["mk_f"], ["mk_b"])
            dv(lambda e: e.tensor_tensor(out=wsT_b[:], in0=wsT_f[:], in1=trilf[:].unsqueeze(1).to_broadcast([128, 8, 128]), op=ALU.mult),
               ["wsT_f", "trilf"], ["wsT_b"])
            ac(lambda e: e.activation(out=esink[:], in_=esink[:], func=AF.Exp), ["esink"], ["esink"])
            dv(lambda e: e.memset(epst[:], EPS), [], ["epst"])
            dv(lambda e: e.memset(qk[:], 0.0), [], ["qk"])
            for i in range(3):
                dv(lambda e, i=i: e.memset(vaug[i][:], 1.0), [], [f"vaug{i}"])
            dv(lambda e: e.memset(VSa[:], 1.0), [], ["VSa"])

            def rms_rstd(src_ap, n, Pn, col, src_res, jk=None, jkn="junk"):
                jk = junk if jk is None else jk
                dv(lambda e: e.memset(st[0:Pn, col:col + 1], 0.0), [], [f"st{col}"])
                ac(lambda e: e.activation(out=jk[0:Pn, 0:n], in_=src_ap, func=AF.Square, accum_out=st[0:Pn, col:col + 1]),
                   src_res, [jkn, f"st{col}"])
                ac(lambda e: e.activation(out=st[0:Pn, col:col + 1], in_=st[0:Pn, col:col + 1], func=AF.Sqrt, scale=1.0 / n, bias=epst[0:Pn, 0:1]),
                   [f"st{col}"], [f"st{col}"])
                dv(lambda e: e.reciprocal(out=st[0:Pn, col:col + 1], in_=st[0:Pn, col:col + 1]), [f"st{col}"], [f"st{col}"])

            def head_rstd(src, nh, Pn, dst_cols, tmp, res_src, res_tmp, sth="st_h"):
                dv(lambda e: e.tensor_tensor(out=tmp[0:Pn, 0:nh * 64], in0=src[0:Pn, 0:nh * 64], in1=src[0:Pn, 0:nh * 64], op=ALU.mult),
                   res_src, res_tmp)
                dv(lambda e: e.tensor_reduce(out=dst_cols, in_=tmp[0:Pn, 0:nh * 64].rearrange("p (h d) -> p h d", d=64), axis=AX.X, op=ALU.add),
                   res_tmp, [sth])
                ac(lambda e: e.activation(out=dst_cols, in_=dst_cols, func=AF.Sqrt, scale=1.0 / 64, bias=epst[0:Pn, 0:1]), [sth], [sth])
                dv(lambda e: e.reciprocal(out=dst_cols, in_=dst_cols), [sth], [sth])

            def transpose8(src_b, Pn, nwt, dstT_fn, res_src, res_dst, tb, nw_res):
                for dc in range(8):
                    pe(lambda e, dc=dc: e.transpose(out=T[tb][:, dc * 128:dc * 128 + Pn], in_=src_b[0:Pn, dc * 128:(dc + 1) * 128],
                                                    identity=identb[0:Pn, 0:Pn]), res_src + ["identb"], [f"T{tb}"])
                dv(lambda e: e.tensor_tensor(out=dstT_fn(), in0=T[tb][:, :].rearrange("p (c t) -> p c t", c=8)[:, :, 0:Pn],
                                             in1=nwt[:, :].unsqueeze(2).to_broadcast([128, 8, Pn]), op=ALU.mult),
                   [nw_res], [f"T{tb}"] + res_dst)

            def front(kind, ti):
                Pn = NS if kind == "sample" else 128
                sl = ti % 2
                s3 = ti % 3
                p3 = (ti - 1) % 3
                QT, ug, gvg = QT2[sl], ug2[sl], gvg2[sl]
                rQT, rug, rgvg = f"QT{sl}", f"ug{sl}", f"gvg{sl}"
                x_res, rt_res = f"xt{sl}", f"rt{sl}"
                if kind == "sample":
                    xsrc, rsrc = xs, rts
                    tok0 = TPC
                else:
                    r0 = 0 if kind == "halo" else 128 * (ti + 1)
                    xsrc, rsrc = xp[r0:r0 + 128, :], rtp[r0:r0 + 128, :]
                    tok0 = 128 * ti
                S.dma("sp", lambda e: e.dma_start(out=xt[sl][0:Pn, :], in_=xsrc), x_res, writes=[x_res])
                S.dma("sp", lambda e: e.dma_start(out=rt[sl][0:Pn, :], in_=rsrc), rt_res, writes=[rt_res])
                rms_rstd(xt[sl][0:Pn, :], 1024, Pn, 0, [x_res])
                dv(lambda e: e.tensor_scalar(out=xn_b[0:Pn, :], in0=xt[sl][0:Pn, :], scalar1=st[0:Pn, 0:1], scalar2=None, op0=ALU.mult),
                   [x_res, "st0"], ["xn_b"])
                transpose8(xn_b, Pn, nwmix, lambda: xnT[:, :, 0:Pn], ["xn_b"], ["xnT"], 0, "nwmix")
                ntile = [(0, 512), (512, 256), (768, 512), (1280, 512)]
                for nt_i, (n0, nw) in enumerate(ntile):
                    if kind == "halo" and nt_i != 1:
                        continue
                    for dc in range(8):
                        pe(lambda e, nt_i=nt_i, n0=n0, nw=nw, dc=dc: e.matmul(P[nt_i][0:Pn, 0:nw], lhsT=xnT[:, dc, 0:Pn], rhs=win_b[:, dc, n0:n0 + nw],
                                                                             start=(dc == 0), stop=(dc == 7)),
                           ["xnT", f"win_b{nt_i}"], [f"P{nt_i}"])
                if kind != "halo":
                    ac(lambda e: e.copy(out=qk[0:Pn, 0:512], in_=P[0][0:Pn, 0:512]), [], ["P0", "qk"])
                ac(lambda e: e.copy(out=qk[0:Pn, 512:640], in_=P[1][0:Pn, 0:128]), [], ["P1", "qk"])
                ac(lambda e: e.copy(out=vf[0:Pn, :], in_=P[1][0:Pn, 128:256]), [], ["P1", "vf"])
                if kind != "halo":
                    ac(lambda e: e.activation(out=ug[0:Pn, :], in_=P[2][0:Pn, :], func=AF.Gelu), [], ["P2", rug])
                    ac(lambda e: e.activation(out=gvg[0:Pn, :], in_=P[3][0:Pn, :], func=AF.Gelu), [], ["P3", rgvg])
                yield 1
                head_rstd(qk, 10, Pn, st[0:Pn, 8:18], qkt, ["qk"], ["qkt"])
                dv(lambda e: e.tensor_tensor(out=qkt[0:Pn, :].rearrange("p (h d) -> p h d", d=64), in0=qk[0:Pn, :].rearrange("p (h d) -> p h d", d=64),
                                             in1=st[0:Pn, 8:18].unsqueeze(2).to_broadcast([Pn, 10, 64]), op=ALU.mult), ["qk", "st_h"], ["qkt"])
                dv(lambda e: e.tensor_tensor(out=qkn[0:Pn, :], in0=qkt[0:Pn, :], in1=nwqk[0:Pn, :], op=ALU.mult), ["qkt", "nwqk"], ["qkn"])
                qv = qkn[0:Pn, :].rearrange("p (h d) -> p h d", d=64)
                dv(lambda e: e.tensor_tensor(out=qkt[0:Pn, :].rearrange("p (h d) -> p h d", d=64), in0=qv,
                                             in1=rt[sl][0:Pn, 0:64].unsqueeze(1).to_broadcast([Pn, 10, 64]), op=ALU.mult), ["qkn", rt_res], ["qkt"])
                sinb = rt[sl][0:Pn, 64:96].unsqueeze(1).to_broadcast([Pn, 10, 32])
                dv(lambda e: e.tensor_tensor(out=rB[0:Pn, 0], in0=qv[:, :, 32:64], in1=sinb, op=ALU.mult), ["qkn", rt_res], ["rB0"])
                dv(lambda e: e.tensor_tensor(out=rB[0:Pn, 1], in0=qv[:, :, 0:32], in1=sinb, op=ALU.mult), ["qkn", rt_res], ["rB1"])
                Av = qkt[0:Pn, :].rearrange("p (h d) -> p h d", d=64)
                qsv = qs[0:Pn, :].rearrange("p (j s d) -> p s j d", j=4, s=2, d=64)
                Aq = qkt[0:Pn, 0:512].rearrange("p (s j d) -> p s j d", s=2, j=4, d=64)
                B0q = rB[0:Pn, 0, 0:8, :].rearrange("p (s j) d -> p s j d", s=2)
                B1q = rB[0:Pn, 1, 0:8, :].rearrange("p (s j) d -> p s j d", s=2)
                if kind != "halo":
                    dv(lambda e: e.tensor_tensor(out=qsv[:, :, :, 0:32], in0=Aq[:, :, :, 0:32], in1=B0q, op=ALU.subtract), ["qkt", "rB0"], ["qs"])
                    dv(lambda e: e.tensor_tensor(out=qsv[:, :, :, 32:64], in0=Aq[:, :, :, 32:64], in1=B1q, op=ALU.add), ["qkt", "rB1"], ["qs"])
                kfv = kf[0:Pn, :].rearrange("p (h d) -> p h d", d=64)
                dv(lambda e: e.tensor_tensor(out=kfv[:, :, 0:32], in0=Av[:, 8:10, 0:32], in1=rB[0:Pn, 0, 8:10, :], op=ALU.subtract), ["qkt", "rB0"], ["kf"])
                dv(lambda e: e.tensor_tensor(out=kfv[:, :, 32:64], in0=Av[:, 8:10, 32:64], in1=rB[0:Pn, 1, 8:10, :], op=ALU.add), ["qkt", "rB1"], ["kf"])
                dv(lambda e: e.tensor_copy(out=ks[0:Pn, :], in_=kf[0:Pn, :]), ["kf"], ["ks"])
                if kind != "sample":
                    dv(lambda e: e.tensor_copy(out=vaug[s3][:, :, 0:64], in_=vf[:, :].rearrange("p (k d) -> p k d", d=64)), ["vf"], [f"vaug{s3}"])
                if kind != "halo":
                    for j in range(4):
                        pe(lambda e, j=j: e.transpose(out=T[1][:, j * 128:j * 128 + Pn], in_=qs[0:Pn, j * 128:(j + 1) * 128], identity=identb[0:Pn, 0:Pn]),
                           ["qs", "identb"], ["T1"])
                if kind != "sample":
                    pe(lambda e: e.transpose(out=T[1][:, 512:640], in_=ks[:, :], identity=identb[:, :]), ["ks", "identb"], ["T1"])
                    ac(lambda e: e.copy(out=KT[s3][:, :], in_=T[1][:, 512:640]), [], ["T1", f"KT{s3}"])
                if kind == "halo":
                    return
                ac(lambda e: e.copy(out=QT[:, :].rearrange("p (j t) -> p j t", j=4)[:, :, 0:Pn],
                                    in_=T[1][:, 0:512].rearrange("p (j t) -> p j t", j=4)[:, :, 0:Pn]), [], ["T1", rQT])

                def spatial():
                    head_rstd(gvg, 8, Pn, st[0:Pn, 20:28], gvt, [rgvg], ["gvt"], "st_g")
                    dv(lambda e: e.tensor_tensor(out=gvt[0:Pn, :].rearrange("p (h d) -> p h d", d=64), in0=gvg[0:Pn, :].rearrange("p (h d) -> p h d", d=64),
                                                 in1=st[0:Pn, 20:28].unsqueeze(2).to_broadcast([Pn, 8, 64]), op=ALU.mult), [rgvg, "st_g"], ["gvt"])
                    dv(lambda e: e.tensor_tensor(out=gvn_f[0:Pn, :], in0=gvt[0:Pn, :], in1=nwgv[0:Pn, :], op=ALU.mult), ["gvt", "nwgv"], ["gvn_f"])
                    if kind == "prompt":
                        dv(lambda e: e.tensor_copy(out=gvn_b[:, :], in_=gvn_f[:, :]), ["gvn_f"], ["gvn_b"])
                        for h in range(8):
                            pe(lambda e, h=h: e.matmul(P[2][:, h * 64:(h + 1) * 64], lhsT=wsT_b[:, h, :], rhs=gvn_b[:, h * 64:(h + 1) * 64], start=True, stop=True),
                               ["wsT_b", "gvn_b"], ["P2"])
                        dv(lambda e: e.tensor_tensor(out=gm[:, :].rearrange("p (h d) -> p h d", d=64), in0=P[2][:, :].rearrange("p (h d) -> p h d", d=64),
                                                     in1=bT[:, :].unsqueeze(2).to_broadcast([128, 8, 64]), op=ALU.add), ["bT"], ["P2", "gm"])
                    else:
                        S.dma("sp", lambda e: e.dma_start(out=ngv_s, in_=gvn_f[0:NS, :]), "ngv_s", reads=["gvn_f"], final=True)
                        dv(lambda e: e.tensor_tensor(out=gm[0:Pn, :].rearrange("p (h d) -> p h d", d=64), in0=gvn_f[0:Pn, :].rearrange("p (h d) -> p h d", d=64),
                                                     in1=ws0[0:Pn, :].unsqueeze(2).to_broadcast([Pn, 8, 64]), op=ALU.mult), ["gvn_f", "ws0"], ["gm"])
                        dv(lambda e: e.tensor_tensor(out=gm[0:Pn, :].rearrange("p (h d) -> p h d", d=64), in0=gm[0:Pn, :].rearrange("p (h d) -> p h d", d=64),
                                                     in1=b0[0:Pn, :].unsqueeze(2).to_broadcast([Pn, 8, 64]), op=ALU.add), ["b0"], ["gm"])

                def gate_mul():
                    dv(lambda e: e.tensor_tensor(out=gm[0:Pn, :], in0=gm[0:Pn, :], in1=ug[0:Pn, :], op=ALU.mult), [rug], ["gm"])

                yield 2
                if kind == "prompt":
                    if ti == NT - 1:
                        S.dma("sp", lambda e: e.dma_start(out=nk_p, in_=kf[:, :]), "nk_p", reads=["kf"], final=True)
                        S.dma("sp", lambda e: e.dma_start(out=nv_p, in_=vf[:, :]), "nv_p", reads=["vf"], final=True)
                    mprev = 0 if ti == 0 else 1
                    for kv in range(2):
                        rr = slice(kv * 64, (kv + 1) * 64)
                        for kt, (ksl, mi) in enumerate(((p3, mprev), (s3, 2))):
                            ix = kv * 2 + kt
                            pe(lambda e, ix=ix, ksl=ksl, rr=rr: e.matmul(P[ix][:, :], lhsT=KT[ksl][rr, :], rhs=QT[rr, :], start=True, stop=True),
                               [f"KT{ksl}", rQT], [f"P{ix}"])
                            ac(lambda e, ix=ix: e.activation(out=PT[:, ix, :], in_=P[ix][:, :], func=AF.Exp, scale=0.125), [], [f"P{ix}", f"PT{ix}"])
                    spatial()
                    yield 3
                    for kv in range(2):
                        for kt, (ksl, mi) in enumerate(((p3, mprev), (s3, 2))):
                            ix = kv * 2 + kt
                            dv(lambda e, ix=ix, mi=mi: e.tensor_tensor(out=PT[:, ix, :].rearrange("p (j t) -> p j t", j=4),
                                                                      in0=PT[:, ix, :].rearrange("p (j t) -> p j t", j=4),
                                                                      in1=mk_b[:, mi, :].unsqueeze(1).to_broadcast([128, 4, 128]), op=ALU.mult),
                               ["mk_b"], [f"PT{ix}"])
                        for j in range(4):
                            for kt, ksl in enumerate((p3, s3)):
                                ix = kv * 2 + kt
                                pe(lambda e, kv=kv, j=j, kt=kt, ksl=ksl, ix=ix: e.matmul(O[kv][:, j * 65:(j + 1) * 65], lhsT=PT[:, ix, j * 128:(j + 1) * 128],
                                                                                        rhs=vaug[ksl][:, kv, :], start=(kt == 0), stop=(kt == 1)),
                                   [f"PT{ix}", f"vaug{ksl}"], [f"O{kv}"])
                    gate_mul()
                else:
                    S.dma("sp", lambda e: e.dma_start(out=nk_s[:, 0:127, :], in_=ck[:, 1:128, :]), "nk_s", writes=["nk_s"], final=True)
                    S.dma("sp", lambda e: e.dma_start(out=nv_s[:, 0:127, :], in_=cv[:, 1:128, :]), "nv_s", writes=["nv_s"], final=True)
                    S.dma("sp", lambda e: e.dma_start(out=nk_s[:, 127, :], in_=kf[0:NS, :]), "nk_s", reads=["kf"], writes=["nk_s2"], final=True)
                    S.dma("sp", lambda e: e.dma_start(out=nv_s[:, 127, :], in_=vf[0:NS, :]), "nv_s", reads=["vf"], writes=["nv_s2"], final=True)
                    S.dma("sp", lambda e: e.dma_start(out=KSj[:, :, :], in_=nk_s.rearrange("b j c -> j b c")), "KSj", reads=["nk_s", "nk_s2"], writes=["KSj"])
                    S.dma("sp", lambda e: e.dma_start(out=VSj[:, :, :], in_=nv_s.rearrange("b j c -> j b c")), "VSj", reads=["nv_s", "nv_s2"], writes=["VSj"])
                    dv(lambda e: e.tensor_copy(out=KSb[:, :, :], in_=KSj[:, :, :]), ["KSj"], ["KSb"])
                    dv(lambda e: e.tensor_copy(out=VSa[:, :, :, 0:64], in_=VSj[:, :, :].rearrange("p b (k d) -> p b k d", d=64)), ["VSj"], ["VSa"])
                    for b in range(NS):
                        tb = b % 2
                        pe(lambda e, b=b: e.transpose(out=T[0][:, (b % 8) * 128:(b % 8 + 1) * 128], in_=KSb[:, b, :], identity=identb[:, :]), ["KSb", "identb"], ["T0"])
                        if b % 8 == 7:
                            g0 = b - 7
                            ac(lambda e, g0=g0: e.copy(out=KTs[:, g0:g0 + 8, :], in_=T[0][:, :].rearrange("p (b j) -> p b j", b=8)), [], ["T0", "KTs"])
                    QTv = QT[:, :].rearrange("p (j t) -> p j t", j=4)
                    for b in range(NS):
                        for kv in range(2):
                            rr = slice(kv * 64, (kv + 1) * 64)
                            pe(lambda e, b=b, kv=kv, rr=rr: e.matmul(P[kv * 2][:, b * 4:b * 4 + 4], lhsT=KTs[rr, b, :], rhs=QTv[rr, :, b],
                                                                    start=True, stop=True), ["KTs", rQT], [f"P{kv * 2}"])
                    PTv = PTs[:, :].rearrange("p (b k j) -> p b k j", k=2, j=4)
                    for kv in range(2):
                        ac(lambda e, kv=kv: e.activation(out=PTv[:, :, kv, :], in_=P[kv * 2][:, 0:64].rearrange("p (b j) -> p b j", j=4), func=AF.Exp, scale=0.125),
                           [], [f"P{kv * 2}", "PTs"])
                    for b in range(NS):
                        for kv in range(2):
                            c0 = b * 8 + kv * 4
                            pe(lambda e, b=b, kv=kv, c0=c0: e.matmul(P[1][0:65, c0:c0 + 4], lhsT=VSa[:, b, kv, :], rhs=PTs[:, c0:c0 + 4], start=True, stop=True),
                               ["VSa", "PTs"], ["P1"])
                    ac(lambda e: e.copy(out=OsT[:, :], in_=P[1][0:65, 0:128]), [], ["P1", "OsT"])
                    OsTv = OsT[:, :].rearrange("p (b h) -> p h b", h=8)
                    for h in range(8):
                        pe(lambda e, h=h: e.transpose(out=O[h // 4][0:NS, (h % 4) * 65:(h % 4 + 1) * 65], in_=OsTv[:, h, :], identity=identf[0:65, 0:65]),
                           ["OsT", "identf"], [f"O{h // 4}"])
                    spatial()
                    gate_mul()
                for kv in range(2):
                    Ov = O[kv][0:Pn, 0:260].rearrange("p (j e) -> p j e", e=65)
                    dv(lambda e, kv=kv, Ov=Ov: e.tensor_tensor(out=den[0:Pn, kv * 4:(kv + 1) * 4], in0=Ov[:, :, 64], in1=esink[0:Pn, kv * 4:(kv + 1) * 4], op=ALU.add),
                       ["esink"], [f"O{kv}", f"den{kv}"])
                    dv(lambda e, kv=kv: e.reciprocal(out=den[0:Pn, kv * 4:(kv + 1) * 4], in_=den[0:Pn, kv * 4:(kv + 1) * 4]), [f"den{kv}"], [f"den{kv}"])
                    dv(lambda e, kv=kv, Ov=Ov: e.tensor_tensor(out=attn[0:Pn, kv * 256:(kv + 1) * 256].rearrange("p (j d) -> p j d", d=64), in0=Ov[:, :, 0:64],
                                                              in1=den[0:Pn, kv * 4:(kv + 1) * 4].unsqueeze(2).to_broadcast([Pn, 4, 64]), op=ALU.mult),
                       [f"den{kv}"], [f"O{kv}", "attn"])
                yield 4
                rms_rstd(attn[0:Pn, :], 512, Pn, 1, ["attn"], junkB, "junkB")
                rms_rstd(gm[0:Pn, :], 512, Pn, 2, ["gm"], junkB, "junkB")
                dv(lambda e: e.tensor_scalar(out=cat_b[0:Pn, 0:512], in0=attn[0:Pn, :], scalar1=st[0:Pn, 1:2], scalar2=None, op0=ALU.mult), ["attn", "st1"], ["cat_b"])
                dv(lambda e: e.tensor_scalar(out=cat_b[0:Pn, 512:1024], in0=gm[0:Pn, :], scalar1=st[0:Pn, 2:3], scalar2=None, op0=ALU.mult), ["gm", "st2"], ["cat_b"])
                transpose8(cat_b, Pn, nwout, lambda: catT[:, :, 0:Pn], ["cat_b"], ["catT"], 0, "nwout")
                for half in range(2):
                    for dc in range(8):
                        pe(lambda e, half=half, dc=dc: e.matmul(P[half][0:Pn, :], lhsT=catT[:, dc, 0:Pn], rhs=wout_b[:, dc, half * 512:(half + 1) * 512],
                                                                start=(dc == 0), stop=(dc == 7)), ["catT", "wout_b"], [f"P{half}"])
                h_res = f"ht{sl}"
                for half in range(2):
                    dv(lambda e, half=half: e.tensor_tensor(out=ht[sl][0:Pn, half * 512:(half + 1) * 512], in0=P[half][0:Pn, :],
                                                            in1=xt[sl][0:Pn, half * 512:(half + 1) * 512], op=ALU.add), [x_res], [f"P{half}", h_res])
                S.dma("sp", lambda e: e.dma_start(out=ytok(tok0, Pn), in_=ht[sl][0:Pn, :]), h_res, reads=[h_res], final=True)
                rms_rstd(ht[sl][0:Pn, :], 1024, Pn, 3, [h_res], junkB, "junkB")
                dv(lambda e: e.tensor_scalar(out=xn2_b[0:Pn, :], in0=ht[sl][0:Pn, :], scalar1=st[0:Pn, 3:4], scalar2=None, op0=ALU.mult),
                   [h_res, "st3"], ["xn2_b"])
                transpose8(xn2_b, Pn, nwffn, lambda: XT[:, :, tok0:tok0 + Pn], ["xn2_b"], [f"XT{tok0}"], 0, "nwffn")

            def adv(g, n=1):
                for _ in range(n):
                    try:
                        next(g)
                    except StopIteration:
                        return
            for _ in front("halo", -1):
                pass
            ntl = min(NT, NTL)
            gens = [front("prompt", ti) for ti in range(ntl)]
            adv(gens[0], 2)
            for ti in range(ntl):
                if ti + 1 < ntl:
                    interleave(lambda ti=ti: adv(gens[ti], 3), lambda ti=ti: adv(gens[ti + 1], 2))
                elif KSAMP:
                    sgen = front("sample", NT)
                    interleave(lambda ti=ti: adv(gens[ti], 3), lambda: adv(sgen, 2))
                else:
                    adv(gens[ti], 3)
            if KSAMP:
                for _ in sgen:
                    pass
            S.emit(final_engine="sp")
        if phases < 2:
            return nc

        with contextlib.ExitStack() as es:
            S = Sched(nc, "b", EXT)
            dv = lambda fn, r=(), w=(): S.op("dve", fn, r, w)
            ac = lambda fn, r=(), w=(): S.op("act", fn, r, w)
            pe = lambda fn, r=(), w=(): S.op("pe", fn, r, w)
            wq_b = sbt(es, "wq_b", [128, 8, 2048], BF16)
            skT_b = sbt(es, "skT_b", [128, 16, 128], BF16)
            iota16 = sbt(es, "iota16", [128, 16], BF16)
            th16 = sbt(es, "th16", [128, 16], F32)
            qpT2 = [sbt(es, f"qpT{i}", [128, 16, 128], BF16) for i in range(2)]
            Ssb2 = [sbt(es, f"Ssb{i}", [128, 16, 128], F32) for i in range(2)]
            wkk = [sbt(es, f"wk{i}", [128, 256], F32) for i in range(2)]
            v16 = sbt(es, "v16", [128, 16, 16], F32)
            i16 = sbt(es, "i16", [128, 16, 16], I32)
            iotai = sbt(es, "iotai", [128, 128], I32)
            i16f = sbt(es, "i16f", [128, 16, 16], BF16)
            cand = sbt(es, "cand", [128, 8, 256], F32)
            comb = sbt(es, "comb", [128, 8, 16, 16], I32)
            best = sbt(es, "best", [128, 8, 16], F32)
            pos = sbt(es, "pos", [128, 8, 16], U32)
            pai = sbt(es, "pai", [128, 8, 16], I32)
            pbi = sbt(es, "pbi", [128, 8, 16], I32)
            paf = sbt(es, "paf", [128, 8, 16], F32)
            pbf = sbt(es, "pbf", [128, 8, 16], BF16)
            eq = sbt(es, "eq", [128, 8, 16, 16], BF16)
            I1f = sbt(es, "I1f", [128, 128], F32)
            I2f = sbt(es, "I2f", [128, 128], F32)
            Gf = sbt(es, "Gf", [128, 128], F32)
            ssum = sbt(es, "ssum", [128, 8], F32)
            Q = [pst(es, f"Q{i}", [128, 512], F32) for i in range(4)]
            Sc = [pst(es, f"Sc{i}", [128, 512], F32) for i in range(4)]
            for g in range(4):
                S.dma("pool", lambda e, g=g: e.dma_start(out=wq_b[:, :, g * 512:(g + 1) * 512], in_=wq_d[:, :, g * 512:(g + 1) * 512]), f"wq_b{g}", writes=[f"wq_b{g}"])
            S.dma("pool", lambda e: e.dma_start(out=skT_b[:, :, :], in_=skT_d), "skT_b", writes=["skT_b"])
            S.dma("sp", lambda e: e.dma_start(out=iotai[:, :], in_=iotai_d), "iotai", writes=["iotai"])
            if phases >= 3:
                G = 4
                for g in range(NCH // G):
                    S.dma("pool", lambda e, g=g: e.dma_start(out=UTs[g * G:(g + 1) * G], in_=UT_d[g * G:(g + 1) * G]), f"x:UTs{g // 8}")
                    S.dma("pool", lambda e, g=g: e.dma_start(out=Vs[g * G:(g + 1) * G], in_=V_d[g * G:(g + 1) * G]), f"x:Vs{g // 8}")
            dv(lambda e: e.tensor_copy(out=iota16[:, :], in_=iotaf[:, 0:16]), [], ["iota16"])
            dv(lambda e: e.tensor_scalar(out=th16[:, :], in0=iotaf[:, 0:16], scalar1=16.0, scalar2=16.0, op0=ALU.mult, op1=ALU.add), [], ["th16"])

            def route(tok0, Pn):
                par = (tok0 // 128) % 2
                qpT = qpT2[par]
                Ssb = Ssb2[par]
                for hp in range(16):
                    for dc in range(8):
                        pe(lambda e, hp=hp, dc=dc: e.matmul(Q[hp // 4][:, (hp % 4) * 128:(hp % 4) * 128 + Pn], lhsT=wq_b[:, dc, hp * 128:(hp + 1) * 128],
                                                            rhs=XT[:, dc, tok0:tok0 + Pn], start=(dc == 0), stop=(dc == 7)), [f"wq_b{hp // 4}"], [f"Q{hp // 4}"])
                for g in range(4):
                    ac(lambda e, g=g: e.copy(out=qpT[:, g * 4:(g + 1) * 4, 0:Pn], in_=Q[g][:, :].rearrange("p (a t) -> p a t", a=4)[:, :, 0:Pn]), [], [f"Q{g}", f"qpT{par}_{g}"])
                for hp in range(16):
                    pe(lambda e, hp=hp: e.matmul(Sc[hp // 4][0:Pn, (hp % 4) * 128:(hp % 4 + 1) * 128], lhsT=qpT[:, hp, 0:Pn], rhs=skT_b[:, hp, :], start=True, stop=True),
                       [f"qpT{par}_{hp // 4}", "skT_b"], [f"Sc{hp // 4}"])
                for g in range(4):
                    ac(lambda e, g=g: e.copy(out=Ssb[0:Pn, g * 4:(g + 1) * 4, :], in_=Sc[g][0:Pn, :].rearrange("p (a k) -> p a k", a=4)), [], [f"Sc{g}", f"Ssb{par}_{g}"])
                yield 1
                SB = [f"Ssb{par}_{g}" for g in range(4)]
                Si = Ssb[0:Pn].bitcast(I32)
                dv(lambda e: e.tensor_single_scalar(out=Si, in_=Si, scalar=-128, op=ALU.bitwise_and), [], SB)
                dv(lambda e: e.tensor_tensor(out=Si, in0=Si, in1=iotai[0:Pn, :].unsqueeze(1).to_broadcast([Pn, 16, 128]), op=ALU.bitwise_or), ["iotai"], SB)
                for hp0 in range(0, 16, 2):
                    ch = [(hp0, 0), (hp0 + 1, 1)]
                    for (hp, w) in ch:
                        dv(lambda e, hp=hp: e.max(out=v16[0:Pn, hp, 0:8], in_=Ssb[0:Pn, hp, :]), [f"Ssb{par}_{hp // 4}"], [f"v16_{hp}"])
                    for (hp, w) in ch:
                        dv(lambda e, hp=hp, w=w: e.match_replace(out=wkk[w][0:Pn, 0:128], in_to_replace=v16[0:Pn, hp, 0:8], in_values=Ssb[0:Pn, hp, :], imm_value=-1e30),
                           [f"Ssb{par}_{hp // 4}", f"v16_{hp}"], [f"wk{w}"])
                    for (hp, w) in ch:
                        dv(lambda e, hp=hp, w=w: e.max(out=v16[0:Pn, hp, 8:16], in_=wkk[w][0:Pn, 0:128]), [f"wk{w}"], [f"v16b_{hp}"])
                ALLV = [f"v16_{hp}" for hp in range(16)] + [f"v16b_{hp}" for hp in range(16)]
                dv(lambda e: e.tensor_single_scalar(out=i16[0:Pn], in_=v16[0:Pn].bitcast(I32), scalar=127, op=ALU.bitwise_and), ALLV, ["i16all"])
                dv(lambda e: e.tensor_copy(out=i16f[0:Pn], in_=i16[0:Pn]), ["i16all"], ["i16f"])
                v16v = v16[0:Pn].rearrange("p (h s) k -> p h s k", s=2)
                dv(lambda e: e.tensor_tensor(out=cand[0:Pn].rearrange("p h (a b) -> p h a b", b=16), in0=v16v[:, :, 0, :].unsqueeze(3).to_broadcast([Pn, 8, 16, 16]),
                                             in1=v16v[:, :, 1, :].unsqueeze(2).to_broadcast([Pn, 8, 16, 16]), op=ALU.add), ALLV, ["cand"])
                i16v = i16f[0:Pn].rearrange("p (h s) k -> p h s k", s=2)
                dv(lambda e: e.tensor_scalar(out=paf[0:Pn], in0=i16v[:, :, 0, :], scalar1=128.0, scalar2=None, op0=ALU.mult), ["i16f"], ["paf"])
                dv(lambda e: e.tensor_tensor(out=comb[0:Pn], in0=paf[0:Pn].unsqueeze(3).to_broadcast([Pn, 8, 16, 16]),
                                             in1=i16v[:, :, 1, :].unsqueeze(2).to_broadcast([Pn, 8, 16, 16]), op=ALU.add), ["i16f", "paf"], ["comb"])
                Ci = cand[0:Pn].bitcast(I32)
                dv(lambda e: e.tensor_single_scalar(out=Ci, in_=Ci, scalar=-16384, op=ALU.bitwise_and), [], ["cand"])
                dv(lambda e: e.tensor_tensor(out=Ci, in0=Ci, in1=comb[0:Pn].rearrange("p h a b -> p h (a b)"), op=ALU.bitwise_or), ["comb"], ["cand"])
                for h0 in range(0, 8, 2):
                    ch = [(h0, 0), (h0 + 1, 1)]
                    for (h, w) in ch:
                        dv(lambda e, h=h: e.max(out=best[0:Pn, h, 0:8], in_=cand[0:Pn, h, :]), ["cand"], [f"best_{h}"])
                    for (h, w) in ch:
                        dv(lambda e, h=h, w=w: e.match_replace(out=wkk[w][0:Pn, :], in_to_replace=best[0:Pn, h, 0:8], in_values=cand[0:Pn, h, :], imm_value=-1e30),
                           ["cand", f"best_{h}"], [f"wk{w}"])
                    for (h, w) in ch:
                        dv(lambda e, h=h, w=w: e.max(out=best[0:Pn, h, 8:16], in_=wkk[w][0:Pn, :]), [f"wk{w}"], [f"bestb_{h}"])
                ALLB = [f"best_{h}" for h in range(8)] + [f"bestb_{h}" for h in range(8)]
                Bi = best[0:Pn].bitcast(I32)
                dv(lambda e: e.tensor_single_scalar(out=pai[0:Pn], in_=Bi, scalar=16383, op=ALU.bitwise_and), ALLB, ["pai"])
                dv(lambda e: e.tensor_single_scalar(out=pbi[0:Pn], in_=pai[0:Pn], scalar=127, op=ALU.bitwise_and), ["pai"], ["pbi"])
                dv(lambda e: e.tensor_single_scalar(out=pai[0:Pn], in_=pai[0:Pn], scalar=7, op=ALU.logical_shift_right), ["pbi"], ["pai"])
                dv(lambda e: e.tensor_copy(out=I1f[0:Pn, :].rearrange("p (h k) -> p h k", k=16), in_=pai[0:Pn]), ["pai"], ["If0"])
                dv(lambda e: e.tensor_copy(out=I2f[0:Pn, :].rearrange("p (h k) -> p h k", k=16), in_=pbi[0:Pn]), ["pbi"], ["If1"])
                dv(lambda e: e.tensor_single_scalar(out=Bi, in_=Bi, scalar=-16384, op=ALU.bitwise_and), ["pai"], ALLB)
                dv(lambda e: e.tensor_tensor(out=Gf[0:Pn, :].rearrange("p (h k) -> p h k", k=16), in0=best[0:Pn], in1=best[0:Pn, :, 0:1].to_broadcast([Pn, 8, 16]),
                                             op=ALU.subtract), ALLB, ["Gf"])
                ac(lambda e: e.activation(out=Gf[0:Pn, :], in_=Gf[0:Pn, :], func=AF.Exp), ["Gf"], ["Gf"])
                dv(lambda e: e.tensor_reduce(out=ssum[0:Pn, :], in_=Gf[0:Pn, :].rearrange("p (h k) -> p h k", k=16), axis=AX.X, op=ALU.add), ["Gf"], ["ssum"])
                dv(lambda e: e.reciprocal(out=ssum[0:Pn, :], in_=ssum[0:Pn, :]), ["ssum"], ["ssum"])
                dv(lambda e: e.tensor_tensor(out=Gf[0:Pn, :].rearrange("p (h k) -> p h k", k=16), in0=Gf[0:Pn, :].rearrange("p (h k) -> p h k", k=16),
                                             in1=ssum[0:Pn, :].unsqueeze(2).to_broadcast([Pn, 8, 16]), op=ALU.mult), ["ssum"], ["Gf"])
                for i, (src, dstT, rn) in enumerate(((I1f, I1T, "If0"), (I2f, I2T, "If1"), (Gf, GT, "Gf"))):
                    pe(lambda e, i=i, src=src: e.transpose(out=Q[i][:, 0:Pn], in_=src[0:Pn, :], identity=identf[0:Pn, 0:Pn]), [rn], [f"Q{i}"])
                    ac(lambda e, i=i, dstT=dstT: e.copy(out=dstT[:, tok0:tok0 + Pn], in_=Q[i][:, 0:Pn]), [], [f"Q{i}", f"T{i}_{tok0}"])

            rgens = [route(ti * 128, 128) for ti in range(NT)] + [route(TPC, NS)]
            next(rgens[0])
            for i in range(len(rgens)):
                if i + 1 < len(rgens):
                    next(rgens[i + 1])
                for _ in rgens[i]:
                    pass
            S.emit(final_engine="sp")
        if phases < 3:
            return nc

        with contextlib.ExitStack() as es:
            S = Sched(nc, "c", EXT)
            dv = lambda fn, r=(), w=(): S.op("dve", fn, r, w)
            ac = lambda fn, r=(), w=(): S.op("act", fn, r, w)
            pe = lambda fn, r=(), w=(): S.op("pe", fn, r, w)
            pl = lambda fn, r=(), w=(): S.op("pool", fn, r, w)
            NR = 4
            WT = sbt(es, "WT", [128, NCH, TT], BF16)
            UTb = [sbt(es, f"UTb{i}", [128, 1024], BF16) for i in range(NR)]
            Vb = [sbt(es, f"Vb{i}", [128, 1024], BF16) for i in range(NR)]
            TG = 16
            NOS = 3
            Ab = [sbt(es, f"Ab{i}", [128, TG, 128], BF16) for i in range(NOS)]
            Bb = [sbt(es, f"Bb{i}", [128, TG, 128], BF16) for i in range(NOS)]
            Gs = [sbt(es, f"Gs{i}", [128, TT], BF16) for i in range(2)]
            HT = [sbt(es, f"HT{i}", [128, TT], BF16) for i in range(2)]
            hb = [sbt(es, f"hb{i}", [128, 1024], F32) for i in range(2)]
            Oa = [pst(es, f"Oa{i}", [128, 512], F32) for i in range(6)]
            A = [pst(es, f"A{i}", [128, 512], F32) for i in range(2)]
            tiles = [(t0, min(TT, NTOK - t0)) for t0 in range(0, NTOK, TT)]
            ci = 0
            oh = 0
            for (t0, tn) in tiles:
                subs = [(s0, min(128, tn - s0)) for s0 in range(0, tn, 128)]
                for tg in range(0, tn, TG):
                    ng = min(TG, tn - tg)
                    sl = oh % NOS
                    oh += 1
                    tb = t0 + tg
                    for k in range(ng):
                        dv(lambda e, sl=sl, t=tb + k, k=k: e.tensor_scalar(out=Bb[sl][:, k, :], in0=iotab[:, :], scalar1=I2T[:, t:t + 1], scalar2=None,
                                                                          op0=ALU.is_equal), [], [f"Bb{sl}_{k // 4}"])
                        dv(lambda e, sl=sl, t=tb + k, k=k: e.tensor_scalar(out=Ab[sl][:, k, :], in0=iotab[:, :], scalar1=I1T[:, t:t + 1], scalar2=GT[:, t:t + 1],
                                                                          op0=ALU.is_equal, op1=ALU.mult), [], [f"Ab{sl}_{k // 4}"])
                    for tq in range(0, ng, 4):
                        nq = min(4, ng - tq)
                        bank = ((tg + tq) // 4) % 2
                        for k in range(nq):
                            pe(lambda e, sl=sl, k=k, tq=tq, bank=bank: e.matmul(A[bank][:, k * 128:(k + 1) * 128], lhsT=Bb[sl][:, tq + k, :], rhs=Ab[sl][:, tq + k, :],
                                                                             start=True, stop=True), [f"Ab{sl}_{tq // 4}", f"Bb{sl}_{tq // 4}"], [f"A{bank}"])
                        ac(lambda e, bank=bank, tq=tq, nq=nq, tg=tg: e.copy(out=WT[:, :, tg + tq:tg + tq + nq],
                                                                          in_=A[bank][:, 0:nq * 128].rearrange("p (t i) -> p i t", t=nq)),
                           [], [f"A{bank}", "WT"])
                def mm2(c, sl, ab):
                    for si, (s0, sn) in enumerate(subs):
                        for half in range(2):
                            pe(lambda e, si=si, s0=s0, sn=sn, half=half, ab=ab, sl=sl, c=c: e.matmul(Oa[si * 2 + half][0:sn, :], lhsT=HT[ab][:, s0:s0 + sn],
                                                                                                  rhs=Vb[sl][:, half * 512:(half + 1) * 512],
                                                                                                  start=(c == 0), stop=(c == NCH - 1)),
                               [f"HT{ab}", f"Vb{sl}"], [f"Oa{si * 2 + half}"])
                prev = None
                for c in range(NCH):
                    sl = ci % NR
                    ab = ci % 2
                    ci += 1
                    xq = c // 32
                    S.dma("sp", lambda e, c=c, sl=sl: e.dma_start(out=UTb[sl][:, :], in_=UTs[c]), f"UTb{sl}", writes=[f"UTb{sl}"],
                          extra_waits=([(f"x:UTs{xq}", 128)] if t0 == 0 else []))
                    S.dma("sp", lambda e, c=c, sl=sl: e.dma_start(out=Vb[sl][:, :], in_=Vs[c]), f"Vb{sl}", writes=[f"Vb{sl}"],
                          extra_waits=([(f"x:Vs{xq}", 128)] if t0 == 0 else []))
                    for dc in range(8):
                        pe(lambda e, dc=dc, sl=sl, ab=ab, t0=t0, tn=tn: e.matmul(A[ab][:, 0:tn], lhsT=UTb[sl][:, dc * 128:(dc + 1) * 128], rhs=XT[:, dc, t0:t0 + tn],
                                                                   start=(dc == 0), stop=(dc == 7)), [f"UTb{sl}"], [f"A{ab}"])
                    ac(lambda e, ab=ab, tn=tn: e.activation(out=Gs[ab][:, 0:tn], in_=A[ab][:, 0:tn], func=AF.Gelu), [], [f"A{ab}", f"Gs{ab}"])
                    dv(lambda e, ab=ab, c=c, tn=tn: e.tensor_tensor(out=HT[ab][:, 0:tn], in0=Gs[ab][:, 0:tn], in1=WT[:, c, 0:tn], op=ALU.mult), [f"Gs{ab}", "WT"], [f"HT{ab}"])
                    if prev is not None:
                        mm2(*prev)
                    prev = (c, sl, ab)
                mm2(*prev)
                for si, (s0, sn) in enumerate(subs):
                    hs = si % 2
                    S.dma("sp", lambda e, hs=hs, s0=s0, sn=sn, t0=t0: e.dma_start(out=hb[hs][0:sn, :], in_=ytok(t0 + s0, sn)), f"hb{hs}", writes=[f"hb{hs}"])
                    for half in range(2):
                        dv(lambda e, hs=hs, sn=sn, si=si, half=half: e.tensor_tensor(out=hb[hs][0:sn, half * 512:(half + 1) * 512], in0=Oa[si * 2 + half][0:sn, :],
                                                                                    in1=hb[hs][0:sn, half * 512:(half + 1) * 512], op=ALU.add),
                           [], [f"Oa{si * 2 + half}", f"hb{hs}"])
                    S.dma("sp", lambda e, hs=hs, s0=s0, sn=sn, t0=t0: e.dma_start(out=ytok(t0 + s0, sn), in_=hb[hs][0:sn, :]), f"hb{hs}", reads=[f"hb{hs}"], final=True)
            S.emit(final_engine="sp")
    return nc


def _host_inputs(x_prompt, x_sample, cache_k, cache_v, norm_mix_w, w_in, q_norm_w, k_norm_w, sinks,
                 gm_v_norm_w, w_spatial, b_spatial, out_norm_w, w_out, norm_ffn_w, w_query, sub_keys,
                 expert_u, expert_v):
    f = np.float32
    rep = lambda v: np.ascontiguousarray(np.broadcast_to(np.asarray(v, f).reshape(1, -1), (128, np.asarray(v).size)))
    col8 = lambda v: np.ascontiguousarray(np.asarray(v, f).reshape(8, 128).T)
    half = 32
    inv = (np.float32(10000.0) ** (-(np.arange(half, dtype=f) / f(half)))).astype(f)

    def rope_tab(pos):
        ang = pos.astype(f)[:, None] * inv[None, :]
        c, s_ = np.cos(ang).astype(f), np.sin(ang).astype(f)
        return np.ascontiguousarray(np.concatenate([c, c, s_], axis=1))

    jj = np.arange(128)[:, None]
    ii = np.arange(128)[None, :]
    m_prev = (jj > ii).astype(f)
    m_own = (jj <= ii).astype(f)
    common = dict(
        ident=np.eye(128, dtype=f), iotai=np.ascontiguousarray(np.broadcast_to(np.arange(128, dtype=np.int32), (128, 128))), iota=np.ascontiguousarray(np.broadcast_to(np.arange(128, dtype=f), (128, 128))),
        tril=(jj <= ii).astype(f),
        nwmix=col8(norm_mix_w[0]), nwout=col8(out_norm_w[0]), nwffn=col8(norm_ffn_w[0]),
        nwqk=rep(np.concatenate([np.tile(q_norm_w[0], 8), np.tile(k_norm_w[0], 2)])),
        nwgv=rep(gm_v_norm_w[0].reshape(-1)), sinksr=rep(sinks[0]),
        ws0=rep(w_spatial[0, :, 0, 0]), b0=rep(b_spatial[0, :, 0]), bT=np.ascontiguousarray(b_spatial[0].T.astype(f)),
        win=np.ascontiguousarray(w_in[0].reshape(8, 128, 1792).transpose(1, 0, 2)),
        wout=np.ascontiguousarray(w_out[0].reshape(8, 128, 1024).transpose(1, 0, 2)),
        wq=np.ascontiguousarray(w_query[0].reshape(8, 128, 2048).transpose(1, 0, 2)),
        skT=np.ascontiguousarray(sub_keys[0].reshape(16, 128, 128).transpose(2, 0, 1)),
        wsT=np.ascontiguousarray(w_spatial[0].transpose(2, 0, 1)),
        UT=np.ascontiguousarray(expert_u[0].reshape(NCH, 128, 8, 128).transpose(0, 3, 2, 1).reshape(NCH, 128, 1024)),
        Vv=np.ascontiguousarray(expert_v[0].reshape(NCH, 128, 1024)),
    )
    maps = []
    xp_all = x_prompt[0]
    for c in range(NCORES):
        lo = c * TPC
        if c == 0:
            xpc = np.concatenate([np.zeros((128, 1024), f), xp_all[0:TPC]], axis=0)
        else:
            xpc = xp_all[lo - 128:lo + TPC]
        pos = np.maximum(np.arange(lo - 128, lo + TPC), 0)
        mk = np.stack([m_prev if c > 0 else np.zeros_like(m_prev), m_prev, m_own])
        d = dict(common)
        d.update(
            xp=np.ascontiguousarray(xpc), xs=np.ascontiguousarray(x_sample[c * NS:(c + 1) * NS, 0, :]),
            ck=np.ascontiguousarray(cache_k[0, c * NS:(c + 1) * NS].reshape(NS, 128, 128)),
            cv=np.ascontiguousarray(cache_v[0, c * NS:(c + 1) * NS].reshape(NS, 128, 128)),
            rtp=rope_tab(pos), rts=rope_tab(np.full((NS,), 16384)), masks=np.ascontiguousarray(mk),
        )
        maps.append(d)
    return maps


def _assemble(results):
    f = np.float32
    y_p = np.concatenate([r["y_p"] for r in results], axis=0).reshape(1, NCORES * TPC, 1024)
    y_s = np.concatenate([r["y_s"] for r in results], axis=0).reshape(NCORES * NS, 1, 1024)
    nk_p = results[-1]["nk_p"].reshape(1, 1, 128, 2, 64)
    nv_p = results[-1]["nv_p"].reshape(1, 1, 128, 2, 64)
    nk_s = np.concatenate([r["nk_s"] for r in results], axis=0).reshape(1, NCORES * NS, 128, 2, 64)
    nv_s = np.concatenate([r["nv_s"] for r in results], axis=0).reshape(1, NCORES * NS, 128, 2, 64)
    ngv = np.concatenate([r["ngv_s"] for r in results], axis=0).reshape(1, NCORES * NS, 1, 8, 64)
    return tuple(np.ascontiguousarray(a.astype(f)) for a in (y_p, y_s, nk_p, nv_p, nk_s, nv_s, ngv))


def kernel(**inputs):
    inputs = {k: np.asarray(v) for k, v in inputs.items()}
    maps = _host_inputs(**inputs)
    nc = build_nc()
    res = run_bass_kernel_spmd(nc, maps, core_ids=list(range(NCORES)))
    return _assemble(res.results)
```

```python
import contextlib
import threading
import numpy as np
import concourse.bass as bass
import concourse.mybir as mybir
from concourse.bass_utils import run_bass_kernel_spmd

F32 = mybir.dt.float32
BF16 = mybir.dt.bfloat16
U32 = mybir.dt.uint32
I32 = mybir.dt.int32
AF = mybir.ActivationFunctionType
ALU = mybir.AluOpType
AX = mybir.AxisListType

NCORES = 8
TPC = 2048
NT = 16
NS = 16
NTOK = TPC + NS
EPS = 1e-6
NCH = 128
TT = 384

ENGS = ("pe", "act", "dve", "pool", "sp")


class Sched:
    EXTCNT = {}

    def __init__(self, nc, tag, ext=None):
        self.nc = nc
        self.tag = tag
        self.ext = ext or {}
        self.streams = {e: [] for e in ENGS}
        self.cnt = {e: 0 for e in ENGS}
        self.waited = {e: {} for e in ENGS}
        self.last_w = {}
        self.readers = {}
        self.dma_cnt = {}
        self.sem_keys = list(ENGS)
        self.final_waits = {}

    def _deps(self, eng, reads, writes):
        deps = []
        for r in reads:
            t = self.last_w.get(r)
            if t is not None:
                deps.append(t)
        for w in writes:
            t = self.last_w.get(w)
            if t is not None:
                deps.append(t)
            deps.extend(self.readers.get(w, ()))
        wd = self.waited[eng]
        best = {}
        for (k, v) in deps:
            if eng == "pe" and k == "pe":
                continue
            if wd.get(k, 0) >= v:
                continue
            if best.get(k, 0) < v:
                best[k] = v
        waits = []
        for k, v in best.items():
            wd[k] = v
            waits.append((k, v))
        return waits

    def _commit(self, tok, reads, writes):
        for r in reads:
            self.readers.setdefault(r, []).append(tok)
        for w in writes:
            self.last_w[w] = tok
            self.readers[w] = []

    tls = threading.local()

    @staticmethod
    def _hook():
        h = getattr(Sched.tls, "hook", None)
        if h is not None and not getattr(Sched.tls, "open", None):
            h()

    @staticmethod
    def _track_banks(eng, writes):
        op = getattr(Sched.tls, "open", None)
        if op is None:
            return
        for w in writes:
            if len(w) == 2 and w[0] in "POT" and w[1].isdigit():
                if eng == "pe":
                    op.add(w)
                else:
                    op.discard(w)

    def op(self, eng, fn, reads=(), writes=()):
        Sched._hook()
        Sched._track_banks(eng, writes)
        waits = self._deps(eng, reads, writes)
        self.cnt[eng] += 1
        tok = (eng, self.cnt[eng])
        self.streams[eng].append((waits, fn, (eng, 1)))
        self._commit(tok, reads, writes)

    def dma(self, queue, fn, semkey, reads=(), writes=(), final=False, extra_waits=()):
        Sched._hook()
        if semkey in self.ext:
            k = semkey
            Sched.EXTCNT[k] = Sched.EXTCNT.get(k, 0) + 1
            waits = self._deps(queue, reads, writes)
            self.streams[queue].append((waits, fn, (k, 16)))
            return
        k = "d:" + semkey
        if k not in self.dma_cnt:
            self.dma_cnt[k] = 0
            self.sem_keys.append(k)
        waits = self._deps(queue, reads, writes) + list(extra_waits)
        self.dma_cnt[k] += 1
        tok = (k, 16 * self.dma_cnt[k])
        self.streams[queue].append((waits, fn, (k, 16)))
        self._commit(tok, reads, writes)
        if final:
            self.final_waits[k] = tok[1]

    def emit(self, final_engine="sp"):
        nc = self.nc
        with contextlib.ExitStack() as es:
            sems = dict(self.ext)
            for i, k in enumerate(self.sem_keys):
                sems[k] = es.enter_context(nc.semaphore(f"{self.tag}_{i}"))
            block = es.enter_context(nc.Block())
            engmap = {"pe": block.tensor, "act": block.scalar, "dve": block.vector,
                      "pool": block.gpsimd, "sp": block.sync}
            for e in ENGS:
                stream = self.streams[e]
                fw = self.final_waits if e == final_engine else {}

                def body(eng, stream=stream, fw=fw):
                    for (waits, fn, (ik, iv)) in stream:
                        for (k, v) in waits:
                            eng.wait_ge(sems[k], v)
                        fn(eng).then_inc(sems[ik], iv)
                    for k, v in fw.items():
                        eng.wait_ge(sems[k], v)

                if stream or fw:
                    engmap[e](body)


import os
STAGE = int(os.environ.get('KSTAGE', '9'))
NTL = int(os.environ.get('KNT', '16'))
KSAMP = int(os.environ.get('KSAMP', '1'))
KSUB = int(os.environ.get('KSUB', '9'))


def interleave(fa, fb):
    turn = [threading.Semaphore(0), threading.Semaphore(0)]
    alive = [True, True]
    err = []

    def runner(i, f):
        turn[i].acquire()

        def h():
            j = 1 - i
            if alive[j]:
                turn[j].release()
                turn[i].acquire()
        Sched.tls.hook = h
        Sched.tls.open = set()
        try:
            f()
        except BaseException as ex:
            err.append(ex)
        finally:
            alive[i] = False
            Sched.tls.hook = None
            Sched.tls.open = None
            turn[1 - i].release()

    ths = [threading.Thread(target=runner, args=(i, f)) for i, f in enumerate((fa, fb))]
    for t in ths:
        t.start()
    turn[0].release()
    for t in ths:
        t.join()
    if err:
        raise err[0]


def build_nc(phases=3, dbg=False):
    nc = bass.Bass("TRN2", target_bir_lowering=False)

    def din(name, shape, dt=F32):
        return nc.dram_tensor(name, list(shape), dt, kind="ExternalInput").ap()

    def dout(name, shape, dt=F32):
        return nc.dram_tensor(name, list(shape), dt, kind="ExternalOutput").ap()

    xp = din("xp", [TPC + 128, 1024])
    xs = din("xs", [NS, 1024])
    ck = din("ck", [NS, 128, 128])
    cv = din("cv", [NS, 128, 128])
    rtp = din("rtp", [TPC + 128, 96])
    rts = din("rts", [NS, 96])
    masks = din("masks", [3, 128, 128])
    ident_d = din("ident", [128, 128])
    iota_d = din("iota", [128, 128])
    iotai_d = din("iotai", [128, 128], I32)
    tril_d = din("tril", [128, 128])
    nwmix_d = din("nwmix", [128, 8])
    nwout_d = din("nwout", [128, 8])
    nwffn_d = din("nwffn", [128, 8])
    nwqk_d = din("nwqk", [128, 640])
    nwgv_d = din("nwgv", [128, 512])
    sinks_d = din("sinksr", [128, 8])
    ws0_d = din("ws0", [128, 8])
    b0_d = din("b0", [128, 8])
    bT_d = din("bT", [128, 8])
    win_d = din("win", [128, 8, 1792])
    wout_d = din("wout", [128, 8, 1024])
    wq_d = din("wq", [128, 8, 2048])
    skT_d = din("skT", [128, 16, 128])
    wsT_d = din("wsT", [128, 8, 128])
    UT_d = din("UT", [NCH, 128, 1024])
    V_d = din("Vv", [NCH, 128, 1024])

    y_p = dout("y_p", [TPC, 1024])
    y_s = dout("y_s", [NS, 1024])
    nk_p = dout("nk_p", [128, 128])
    nv_p = dout("nv_p", [128, 128])
    nk_s = dout("nk_s", [NS, 128, 128])
    nv_s = dout("nv_s", [NS, 128, 128])
    ngv_s = dout("ngv_s", [NS, 512])

    UTs = nc.dram_tensor("UTs", [NCH, 128, 1024], BF16, kind="Internal").ap()
    Vs = nc.dram_tensor("Vs", [NCH, 128, 1024], BF16, kind="Internal").ap()

    def ytok(t0, n):
        if t0 >= TPC:
            return y_s[t0 - TPC:t0 - TPC + n, :]
        return y_p[t0:t0 + n, :]

    with contextlib.ExitStack() as top:
        def sbt(es, name, shape, dt):
            return es.enter_context(nc.sbuf_tensor("s_" + name, list(shape), dt))

        def pst(es, name, shape, dt):
            return es.enter_context(nc.psum_tensor("p_" + name, list(shape), dt))

        XT = sbt(top, "XT", [128, 8, NTOK], BF16)
        I1T = sbt(top, "I1T", [128, NTOK], BF16)
        I2T = sbt(top, "I2T", [128, NTOK], BF16)
        GT = sbt(top, "GT", [128, NTOK], BF16)
        identf = sbt(top, "identf", [128, 128], F32)
        identb = sbt(top, "identb", [128, 128], BF16)
        iotaf = sbt(top, "iotaf", [128, 128], F32)
        iotab = sbt(top, "iotab", [128, 128], BF16)
        Sched.EXTCNT.clear()
        EXT = {}
        for qq in range(4):
            EXT[f"x:UTs{qq}"] = top.enter_context(nc.semaphore(f"xUTs{qq}"))
            EXT[f"x:Vs{qq}"] = top.enter_context(nc.semaphore(f"xVs{qq}"))

        with contextlib.ExitStack() as es:
            S = Sched(nc, "a")
            dv = lambda fn, r=(), w=(): S.op("dve", fn, r, w)
            ac = lambda fn, r=(), w=(): S.op("act", fn, r, w)
            pe = lambda fn, r=(), w=(): S.op("pe", fn, r, w)
            pl = lambda fn, r=(), w=(): S.op("pool", fn, r, w)

            win_b = sbt(es, "win_b", [128, 8, 1792], BF16)
            wout_b = sbt(es, "wout_b", [128, 8, 1024], BF16)
            wsT_f = sbt(es, "wsT_f", [128, 8, 128], F32)
            wsT_b = sbt(es, "wsT_b", [128, 8, 128], BF16)
            trilf = sbt(es, "trilf", [128, 128], F32)
            mk_f = sbt(es, "mk_f", [128, 3, 128], F32)
            mk_b = sbt(es, "mk_b", [128, 3, 128], BF16)
            nwmix = sbt(es, "nwmix", [128, 8], F32)
            nwout = sbt(es, "nwout", [128, 8], F32)
            nwffn = sbt(es, "nwffn", [128, 8], F32)
            nwqk = sbt(es, "nwqk", [128, 640], F32)
            nwgv = sbt(es, "nwgv", [128, 512], F32)
            esink = sbt(es, "esink", [128, 8], F32)
            ws0 = sbt(es, "ws0", [128, 8], F32)
            b0 = sbt(es, "b0", [128, 8], F32)
            bT = sbt(es, "bT", [128, 8], F32)
            epst = sbt(es, "epst", [128, 1], F32)

            xt = [sbt(es, f"xt{i}", [128, 1024], F32) for i in range(2)]
            rt = [sbt(es, f"rt{i}", [128, 96], F32) for i in range(2)]
            junk = sbt(es, "junk", [128, 1024], BF16)
            st = sbt(es, "st", [128, 64], F32)
            xn_b = sbt(es, "xn_b", [128, 1024], BF16)
            xnT = sbt(es, "xnT", [128, 8, 128], BF16)
            qk = sbt(es, "qk", [128, 640], F32)
            qkt = sbt(es, "qkt", [128, 640], F32)
            qkn = sbt(es, "qkn", [128, 640], F32)
            rB = sbt(es, "rB", [128, 2, 10, 32], F32)
            qs = sbt(es, "qs", [128, 512], BF16)
            ks = sbt(es, "ks", [128, 128], BF16)
            kf = sbt(es, "kf", [128, 128], F32)
            vf = sbt(es, "vf", [128, 128], F32)
            vaug = [sbt(es, f"vaug{i}", [128, 2, 65], BF16) for i in range(3)]
            KT = [sbt(es, f"KT{i}", [128, 128], BF16) for i in range(3)]
            QT2 = [sbt(es, f"QT{i}", [128, 512], BF16) for i in range(2)]
            PT = sbt(es, "PT", [128, 4, 512], BF16)
            den = sbt(es, "den", [128, 8], F32)
            attn = sbt(es, "attn", [128, 512], F32)
            ug2 = [sbt(es, f"ug{i}", [128, 512], F32) for i in range(2)]
            gvg2 = [sbt(es, f"gvg{i}", [128, 512], F32) for i in range(2)]
            junkB = sbt(es, "junkB", [128, 1024], BF16)
            xn2_b = sbt(es, "xn2_b", [128, 1024], BF16)
            gvt = sbt(es, "gvt", [128, 512], F32)
            gvn_f = sbt(es, "gvn_f", [128, 512], F32)
            gvn_b = sbt(es, "gvn_b", [128, 512], BF16)
            gm = sbt(es, "gm", [128, 512], F32)
            cat_b = sbt(es, "cat_b", [128, 1024], BF16)
            catT = sbt(es, "catT", [128, 8, 128], BF16)
            ht = [sbt(es, f"ht{i}", [128, 1024], F32) for i in range(2)]
            KSj = sbt(es, "KSj", [128, NS, 128], F32)
            VSj = sbt(es, "VSj", [128, NS, 128], F32)
            KSb = sbt(es, "KSb", [128, NS, 128], BF16)
            KTs = sbt(es, "KTs", [128, NS, 128], BF16)
            VSa = sbt(es, "VSa", [128, NS, 2, 65], BF16)
            PTs = sbt(es, "PTs", [128, 128], BF16)
            OsT = sbt(es, "OsT", [65, 128], F32)

            P = [pst(es, f"P{i}", [128, 512], F32) for i in range(4)]
            O = [pst(es, f"O{i}", [128, 512], F32) for i in range(2)]
            T = [pst(es, f"T{i}", [128, 1024], BF16) for i in range(2)]

            def ld(dst, src, name, q="sp"):
                S.dma(q, lambda e: e.dma_start(out=dst, in_=src), name, writes=[name])

            ld(identf[:], ident_d, "identf")
            ld(iotaf[:], iota_d, "iotaf")
            ld(trilf[:], tril_d, "trilf")
            ld(mk_f[:], masks.rearrange("m j i -> j m i"), "mk_f")
            ld(nwmix[:], nwmix_d, "nwmix")
            ld(nwout[:], nwout_d, "nwout")
            ld(nwffn[:], nwffn_d, "nwffn")
            ld(nwqk[:], nwqk_d, "nwqk")
            ld(nwgv[:], nwgv_d, "nwgv")
            ld(esink[:], sinks_d, "esink")
            ld(ws0[:], ws0_d, "ws0")
            ld(b0[:], b0_d, "b0")
            ld(bT[:], bT_d, "bT")
            ld(wsT_f[:], wsT_d, "wsT_f")
            for nt_i, (n0, nw) in ((1, (512, 256)), (0, (0, 512)), (2, (768, 512)), (3, (1280, 512))):
                S.dma("pool", lambda e, n0=n0, nw=nw: e.dma_start(out=win_b[:, :, n0:n0 + nw], in_=win_d[:, :, n0:n0 + nw]), f"win_b{nt_i}", writes=[f"win_b{nt_i}"])
            for dc in range(8):
                S.dma("pool", lambda e, dc=dc: e.dma_start(out=wout_b[:, dc, :], in_=wout_d[:, dc, :]), "wout_b", writes=["wout_b"])
            dv(lambda e: e.tensor_copy(out=identb[:], in_=identf[:]), ["identf"], ["identb"])
            dv(lambda e: e.tensor_copy(out=iotab[:], in_=iotaf[:]), ["iotaf"], ["iotab"])
            dv(lambda e: e.tensor_copy(out=mk_b[:], in_=mk_f[:]), ["mk_f"], ["mk_b"])
            dv(lambda e: e.tensor_tensor(out=wsT_b[:], in0=wsT_f[:], in1=trilf[:].unsqueeze(1).to_broadcast([128, 8, 128]), op=ALU.mult),
               ["wsT_f", "trilf"], ["wsT_b"])
            ac(lambda e: e.activation(out=esink[:], in_=esink[:], func=AF.Exp), ["esink"], ["esink"])
            dv(lambda e: e.memset(epst[:], EPS), [], ["epst"])
            dv(lambda e: e.memset(qk[:], 0.0), [], ["qk"])
            for i in range(3):
                dv(lambda e, i=i: e.memset(vaug[i][:], 1.0), [], [f"vaug{i}"])
            dv(lambda e: e.memset(VSa[:], 1.0), [], ["VSa"])

            def rms_rstd(src_ap, n, Pn, col, src_res, jk=None, jkn="junk"):
                jk = junk if jk is None else jk
                dv(lambda e: e.memset(st[0:Pn, col:col + 1], 0.0), [], [f"st{col}"])
                ac(lambda e: e.activation(out=jk[0:Pn, 0:n], in_=src_ap, func=AF.Square, accum_out=st[0:Pn, col:col + 1]),
                   src_res, [jkn, f"st{col}"])
                ac(lambda e: e.activation(out=st[0:Pn, col:col + 1], in_=st[0:Pn, col:col + 1], func=AF.Sqrt, scale=1.0 / n, bias=epst[0:Pn, 0:1]),
                   [f"st{col}"], [f"st{col}"])
                dv(lambda e: e.reciprocal(out=st[0:Pn, col:col + 1], in_=st[0:Pn, col:col + 1]), [f"st{col}"], [f"st{col}"])

            def head_rstd(src, nh, Pn, dst_cols, tmp, res_src, res_tmp, sth="st_h"):
                dv(lambda e: e.tensor_tensor(out=tmp[0:Pn, 0:nh * 64], in0=src[0:Pn, 0:nh * 64], in1=src[0:Pn, 0:nh * 64], op=ALU.mult),
                   res_src, res_tmp)
                dv(lambda e: e.tensor_reduce(out=dst_cols, in_=tmp[0:Pn, 0:nh * 64].rearrange("p (h d) -> p h d", d=64), axis=AX.X, op=ALU.add),
                   res_tmp, [sth])
                ac(lambda e: e.activation(out=dst_cols, in_=dst_cols, func=AF.Sqrt, scale=1.0 / 64, bias=epst[0:Pn, 0:1]), [sth], [sth])
                dv(lambda e: e.reciprocal(out=dst_cols, in_=dst_cols), [sth], [sth])

            def transpose8(src_b, Pn, nwt, dstT_fn, res_src, res_dst, tb, nw_res):
                for dc in range(8):
                    pe(lambda e, dc=dc: e.transpose(out=T[tb][:, dc * 128:dc * 128 + Pn], in_=src_b[0:Pn, dc * 128:(dc + 1) * 128],
                                                    identity=identb[0:Pn, 0:Pn]), res_src + ["identb"], [f"T{tb}"])
                dv(lambda e: e.tensor_tensor(out=dstT_fn(), in0=T[tb][:, :].rearrange("p (c t) -> p c t", c=8)[:, :, 0:Pn],
                                             in1=nwt[:, :].unsqueeze(2).to_broadcast([128, 8, Pn]), op=ALU.mult),
                   [nw_res], [f"T{tb}"] + res_dst)

            def front(kind, ti):
                Pn = NS if kind == "sample" else 128
                sl = ti % 2
                s3 = ti % 3
                p3 = (ti - 1) % 3
                QT, ug, gvg = QT2[sl], ug2[sl], gvg2[sl]
                rQT, rug, rgvg = f"QT{sl}", f"ug{sl}", f"gvg{sl}"
                x_res, rt_res = f"xt{sl}", f"rt{sl}"
                if kind == "sample":
                    xsrc, rsrc = xs, rts
                    tok0 = TPC
                else:
                    r0 = 0 if kind == "halo" else 128 * (ti + 1)
                    xsrc, rsrc = xp[r0:r0 + 128, :], rtp[r0:r0 + 128, :]
                    tok0 = 128 * ti
                S.dma("sp", lambda e: e.dma_start(out=xt[sl][0:Pn, :], in_=xsrc), x_res, writes=[x_res])
                S.dma("sp", lambda e: e.dma_start(out=rt[sl][0:Pn, :], in_=rsrc), rt_res, writes=[rt_res])
                rms_rstd(xt[sl][0:Pn, :], 1024, Pn, 0, [x_res])
                dv(lambda e: e.tensor_scalar(out=xn_b[0:Pn, :], in0=xt[sl][0:Pn, :], scalar1=st[0:Pn, 0:1], scalar2=None, op0=ALU.mult),
                   [x_res, "st0"], ["xn_b"])
                transpose8(xn_b, Pn, nwmix, lambda: xnT[:, :, 0:Pn], ["xn_b"], ["xnT"], 0, "nwmix")
                ntile = [(0, 512), (512, 256), (768, 512), (1280, 512)]
                for nt_i, (n0, nw) in enumerate(ntile):
                    if kind == "halo" and nt_i != 1:
                        continue
                    for dc in range(8):
                        pe(lambda e, nt_i=nt_i, n0=n0, nw=nw, dc=dc: e.matmul(P[nt_i][0:Pn, 0:nw], lhsT=xnT[:, dc, 0:Pn], rhs=win_b[:, dc, n0:n0 + nw],
                                                                             start=(dc == 0), stop=(dc == 7)),
                           ["xnT", f"win_b{nt_i}"], [f"P{nt_i}"])
                if kind != "halo":
                    ac(lambda e: e.copy(out=qk[0:Pn, 0:512], in_=P[0][0:Pn, 0:512]), [], ["P0", "qk"])
                ac(lambda e: e.copy(out=qk[0:Pn, 512:640], in_=P[1][0:Pn, 0:128]), [], ["P1", "qk"])
                ac(lambda e: e.copy(out=vf[0:Pn, :], in_=P[1][0:Pn, 128:256]), [], ["P1", "vf"])
                if kind != "halo":
                    ac(lambda e: e.activation(out=ug[0:Pn, :], in_=P[2][0:Pn, :], func=AF.Gelu), [], ["P2", rug])
                    ac(lambda e: e.activation(out=gvg[0:Pn, :], in_=P[3][0:Pn, :], func=AF.Gelu), [], ["P3", rgvg])
                yield 1
                head_rstd(qk, 10, Pn, st[0:Pn, 8:18], qkt, ["qk"], ["qkt"])
                dv(lambda e: e.tensor_tensor(out=qkt[0:Pn, :].rearrange("p (h d) -> p h d", d=64), in0=qk[0:Pn, :].rearrange("p (h d) -> p h d", d=64),
                                             in1=st[0:Pn, 8:18].unsqueeze(2).to_broadcast([Pn, 10, 64]), op=ALU.mult), ["qk", "st_h"], ["qkt"])
                dv(lambda e: e.tensor_tensor(out=qkn[0:Pn, :], in0=qkt[0:Pn, :], in1=nwqk[0:Pn, :], op=ALU.mult), ["qkt", "nwqk"], ["qkn"])
                qv = qkn[0:Pn, :].rearrange("p (h d) -> p h d", d=64)
                dv(lambda e: e.tensor_tensor(out=qkt[0:Pn, :].rearrange("p (h d) -> p h d", d=64), in0=qv,
                                             in1=rt[sl][0:Pn, 0:64].unsqueeze(1).to_broadcast([Pn, 10, 64]), op=ALU.mult), ["qkn", rt_res], ["qkt"])
                sinb = rt[sl][0:Pn, 64:96].unsqueeze(1).to_broadcast([Pn, 10, 32])
                dv(lambda e: e.tensor_tensor(out=rB[0:Pn, 0], in0=qv[:, :, 32:64], in1=sinb, op=ALU.mult), ["qkn", rt_res], ["rB0"])
                dv(lambda e: e.tensor_tensor(out=rB[0:Pn, 1], in0=qv[:, :, 0:32], in1=sinb, op=ALU.mult), ["qkn", rt_res], ["rB1"])
                Av = qkt[0:Pn, :].rearrange("p (h d) -> p h d", d=64)
                qsv = qs[0:Pn, :].rearrange("p (j s d) -> p s j d", j=4, s=2, d=64)
                Aq = qkt[0:Pn, 0:512].rearrange("p (s j d) -> p s j d", s=2, j=4, d=64)
                B0q = rB[0:Pn, 0, 0:8, :].rearrange("p (s j) d -> p s j d", s=2)
                B1q = rB[0:Pn, 1, 0:8, :].rearrange("p (s j) d -> p s j d", s=2)
                if kind != "halo":
                    dv(lambda e: e.tensor_tensor(out=qsv[:, :, :, 0:32], in0=Aq[:, :, :, 0:32], in1=B0q, op=ALU.subtract), ["qkt", "rB0"], ["qs"])
                    dv(lambda e: e.tensor_tensor(out=qsv[:, :, :, 32:64], in0=Aq[:, :, :, 32:64], in1=B1q, op=ALU.add), ["qkt", "rB1"], ["qs"])
                kfv = kf[0:Pn, :].rearrange("p (h d) -> p h d", d=64)
                dv(lambda e: e.tensor_tensor(out=kfv[:, :, 0:32], in0=Av[:, 8:10, 0:32], in1=rB[0:Pn, 0, 8:10, :], op=ALU.subtract), ["qkt", "rB0"], ["kf"])
                dv(lambda e: e.tensor_tensor(out=kfv[:, :, 32:64], in0=Av[:, 8:10, 32:64], in1=rB[0:Pn, 1, 8:10, :], op=ALU.add), ["qkt", "rB1"], ["kf"])
                dv(lambda e: e.tensor_copy(out=ks[0:Pn, :], in_=kf[0:Pn, :]), ["kf"], ["ks"])
                if kind != "sample":
                    dv(lambda e: e.tensor_copy(out=vaug[s3][:, :, 0:64], in_=vf[:, :].rearrange("p (k d) -> p k d", d=64)), ["vf"], [f"vaug{s3}"])
                if kind != "halo":
                    for j in range(4):
                        pe(lambda e, j=j: e.transpose(out=T[1][:, j * 128:j * 128 + Pn], in_=qs[0:Pn, j * 128:(j + 1) * 128], identity=identb[0:Pn, 0:Pn]),
                           ["qs", "identb"], ["T1"])
                if kind != "sample":
                    pe(lambda e: e.transpose(out=T[1][:, 512:640], in_=ks[:, :], identity=identb[:, :]), ["ks", "identb"], ["T1"])
                    ac(lambda e: e.copy(out=KT[s3][:, :], in_=T[1][:, 512:640]), [], ["T1", f"KT{s3}"])
                if kind == "halo":
                    return
                ac(lambda e: e.copy(out=QT[:, :].rearrange("p (j t) -> p j t", j=4)[:, :, 0:Pn],
                                    in_=T[1][:, 0:512].rearrange("p (j t) -> p j t", j=4)[:, :, 0:Pn]), [], ["T1", rQT])

                def spatial():
                    head_rstd(gvg, 8, Pn, st[0:Pn, 20:28], gvt, [rgvg], ["gvt"], "st_g")
                    dv(lambda e: e.tensor_tensor(out=gvt[0:Pn, :].rearrange("p (h d) -> p h d", d=64), in0=gvg[0:Pn, :].rearrange("p (h d) -> p h d", d=64),
                                                 in1=st[0:Pn, 20:28].unsqueeze(2).to_broadcast([Pn, 8, 64]), op=ALU.mult), [rgvg, "st_g"], ["gvt"])
                    dv(lambda e: e.tensor_tensor(out=gvn_f[0:Pn, :], in0=gvt[0:Pn, :], in1=nwgv[0:Pn, :], op=ALU.mult), ["gvt", "nwgv"], ["gvn_f"])
                    if kind == "prompt":
                        dv(lambda e: e.tensor_copy(out=gvn_b[:, :], in_=gvn_f[:, :]), ["gvn_f"], ["gvn_b"])
                        for h in range(8):
                            pe(lambda e, h=h: e.matmul(P[2][:, h * 64:(h + 1) * 64], lhsT=wsT_b[:, h, :], rhs=gvn_b[:, h * 64:(h + 1) * 64], start=True, stop=True),
                               ["wsT_b", "gvn_b"], ["P2"])
                    else:
                        S.dma("sp", lambda e: e.dma_start(out=ngv_s, in_=gvn_f[0:NS, :]), "ngv_s", reads=["gvn_f"], final=True)
                        dv(lambda e: e.tensor_tensor(out=gm[0:Pn, :].rearrange("p (h d) -> p h d", d=64), in0=gvn_f[0:Pn, :].rearrange("p (h d) -> p h d", d=64),
                                                     in1=ws0[0:Pn, :].unsqueeze(2).to_broadcast([Pn, 8, 64]), op=ALU.mult), ["gvn_f", "ws0"], ["gm"])
                        dv(lambda e: e.tensor_tensor(out=gm[0:Pn, :].rearrange("p (h d) -> p h d", d=64), in0=gm[0:Pn, :].rearrange("p (h d) -> p h d", d=64),
                                                     in1=b0[0:Pn, :].unsqueeze(2).to_broadcast([Pn, 8, 64]), op=ALU.add), ["b0"], ["gm"])

                def gate_mul():
                    if kind == "prompt":
                        dv(lambda e: e.tensor_tensor(out=gm[:, :].rearrange("p (h d) -> p h d", d=64), in0=P[2][:, :].rearrange("p (h d) -> p h d", d=64),
                                                     in1=bT[:, :].unsqueeze(2).to_broadcast([128, 8, 64]), op=ALU.add), ["bT"], ["P2", "gm"])
                    dv(lambda e: e.tensor_tensor(out=gm[0:Pn, :], in0=gm[0:Pn, :], in1=ug[0:Pn, :], op=ALU.mult), [rug], ["gm"])

                yield 2
                if kind == "prompt":
                    if ti == NT - 1:
                        S.dma("sp", lambda e: e.dma_start(out=nk_p, in_=kf[:, :]), "nk_p", reads=["kf"], final=True)
                        S.dma("sp", lambda e: e.dma_start(out=nv_p, in_=vf[:, :]), "nv_p", reads=["vf"], final=True)
                    mprev = 0 if ti == 0 else 1
                    for kv in range(2):
                        rr = slice(kv * 64, (kv + 1) * 64)
                        for kt, (ksl, mi) in enumerate(((p3, mprev), (s3, 2))):
                            ix = kv * 2 + kt
                            pe(lambda e, ix=ix, ksl=ksl, rr=rr: e.matmul(P[ix][:, :], lhsT=KT[ksl][rr, :], rhs=QT[rr, :], start=True, stop=True),
                               [f"KT{ksl}", rQT], [f"P{ix}"])
                            ac(lambda e, ix=ix: e.activation(out=PT[:, ix, :], in_=P[ix][:, :], func=AF.Exp, scale=0.125), [], [f"P{ix}", f"PT{ix}"])
                    spatial()
                    yield 3
                    for kv in range(2):
                        for kt, (ksl, mi) in enumerate(((p3, mprev), (s3, 2))):
                            ix = kv * 2 + kt
                            dv(lambda e, ix=ix, mi=mi: e.tensor_tensor(out=PT[:, ix, :].rearrange("p (j t) -> p j t", j=4),
                                                                      in0=PT[:, ix, :].rearrange("p (j t) -> p j t", j=4),
                                                                      in1=mk_b[:, mi, :].unsqueeze(1).to_broadcast([128, 4, 128]), op=ALU.mult),
                               ["mk_b"], [f"PT{ix}"])
                        for j in range(4):
                            for kt, ksl in enumerate((p3, s3)):
                                ix = kv * 2 + kt
                                pe(lambda e, kv=kv, j=j, kt=kt, ksl=ksl, ix=ix: e.matmul(O[kv][:, j * 65:(j + 1) * 65], lhsT=PT[:, ix, j * 128:(j + 1) * 128],
                                                                                        rhs=vaug[ksl][:, kv, :], start=(kt == 0), stop=(kt == 1)),
                                   [f"PT{ix}", f"vaug{ksl}"], [f"O{kv}"])
                    gate_mul()
                else:
                    S.dma("sp", lambda e: e.dma_start(out=nk_s[:, 0:127, :], in_=ck[:, 1:128, :]), "nk_s", writes=["nk_s"], final=True)
                    S.dma("sp", lambda e: e.dma_start(out=nv_s[:, 0:127, :], in_=cv[:, 1:128, :]), "nv_s", writes=["nv_s"], final=True)
                    S.dma("sp", lambda e: e.dma_start(out=nk_s[:, 127, :], in_=kf[0:NS, :]), "nk_s", reads=["kf"], writes=["nk_s2"], final=True)
                    S.dma("sp", lambda e: e.dma_start(out=nv_s[:, 127, :], in_=vf[0:NS, :]), "nv_s", reads=["vf"], writes=["nv_s2"], final=True)
                    S.dma("sp", lambda e: e.dma_start(out=KSj[:, :, :], in_=nk_s.rearrange("b j c -> j b c")), "KSj", reads=["nk_s", "nk_s2"], writes=["KSj"])
                    S.dma("sp", lambda e: e.dma_start(out=VSj[:, :, :], in_=nv_s.rearrange("b j c -> j b c")), "VSj", reads=["nv_s", "nv_s2"], writes=["VSj"])
                    dv(lambda e: e.tensor_copy(out=KSb[:, :, :], in_=KSj[:, :, :]), ["KSj"], ["KSb"])
                    dv(lambda e: e.tensor_copy(out=VSa[:, :, :, 0:64], in_=VSj[:, :, :].rearrange("p b (k d) -> p b k d", d=64)), ["VSj"], ["VSa"])
                    for b in range(NS):
                        tb = b % 2
                        pe(lambda e, b=b: e.transpose(out=T[0][:, (b % 8) * 128:(b % 8 + 1) * 128], in_=KSb[:, b, :], identity=identb[:, :]), ["KSb", "identb"], ["T0"])
                        if b % 8 == 7:
                            g0 = b - 7
                            ac(lambda e, g0=g0: e.copy(out=KTs[:, g0:g0 + 8, :], in_=T[0][:, :].rearrange("p (b j) -> p b j", b=8)), [], ["T0", "KTs"])
                    QTv = QT[:, :].rearrange("p (j t) -> p j t", j=4)
                    for b in range(NS):
                        for kv in range(2):
                            rr = slice(kv * 64, (kv + 1) * 64)
                            pe(lambda e, b=b, kv=kv, rr=rr: e.matmul(P[kv * 2][:, b * 4:b * 4 + 4], lhsT=KTs[rr, b, :], rhs=QTv[rr, :, b],
                                                                    start=True, stop=True), ["KTs", rQT], [f"P{kv * 2}"])
                    PTv = PTs[:, :].rearrange("p (b k j) -> p b k j", k=2, j=4)
                    for kv in range(2):
                        ac(lambda e, kv=kv: e.activation(out=PTv[:, :, kv, :], in_=P[kv * 2][:, 0:64].rearrange("p (b j) -> p b j", j=4), func=AF.Exp, scale=0.125),
                           [], [f"P{kv * 2}", "PTs"])
                    for b in range(NS):
                        for kv in range(2):
                            c0 = b * 8 + kv * 4
                            pe(lambda e, b=b, kv=kv, c0=c0: e.matmul(P[1][0:65, c0:c0 + 4], lhsT=VSa[:, b, kv, :], rhs=PTs[:, c0:c0 + 4], start=True, stop=True),
                               ["VSa", "PTs"], ["P1"])
                    ac(lambda e: e.copy(out=OsT[:, :], in_=P[1][0:65, 0:128]), [], ["P1", "OsT"])
                    OsTv = OsT[:, :].rearrange("p (b h) -> p h b", h=8)
                    for h in range(8):
                        pe(lambda e, h=h: e.transpose(out=O[h // 4][0:NS, (h % 4) * 65:(h % 4 + 1) * 65], in_=OsTv[:, h, :], identity=identf[0:65, 0:65]),
                           ["OsT", "identf"], [f"O{h // 4}"])
                    spatial()
                    gate_mul()
                for kv in range(2):
                    Ov = O[kv][0:Pn, 0:260].rearrange("p (j e) -> p j e", e=65)
                    dv(lambda e, kv=kv, Ov=Ov: e.tensor_tensor(out=den[0:Pn, kv * 4:(kv + 1) * 4], in0=Ov[:, :, 64], in1=esink[0:Pn, kv * 4:(kv + 1) * 4], op=ALU.add),
                       ["esink"], [f"O{kv}", f"den{kv}"])
                    dv(lambda e, kv=kv: e.reciprocal(out=den[0:Pn, kv * 4:(kv + 1) * 4], in_=den[0:Pn, kv * 4:(kv + 1) * 4]), [f"den{kv}"], [f"den{kv}"])
                    dv(lambda e, kv=kv, Ov=Ov: e.tensor_tensor(out=attn[0:Pn, kv * 256:(kv + 1) * 256].rearrange("p (j d) -> p j d", d=64), in0=Ov[:, :, 0:64],
                                                              in1=den[0:Pn, kv * 4:(kv + 1) * 4].unsqueeze(2).to_broadcast([Pn, 4, 64]), op=ALU.mult),
                       [f"den{kv}"], [f"O{kv}", "attn"])
                yield 4
                rms_rstd(attn[0:Pn, :], 512, Pn, 1, ["attn"], junkB, "junkB")
                rms_rstd(gm[0:Pn, :], 512, Pn, 2, ["gm"], junkB, "junkB")
                dv(lambda e: e.tensor_scalar(out=cat_b[0:Pn, 0:512], in0=attn[0:Pn, :], scalar1=st[0:Pn, 1:2], scalar2=None, op0=ALU.mult), ["attn", "st1"], ["cat_b"])
                dv(lambda e: e.tensor_scalar(out=cat_b[0:Pn, 512:1024], in0=gm[0:Pn, :], scalar1=st[0:Pn, 2:3], scalar2=None, op0=ALU.mult), ["gm", "st2"], ["cat_b"])
                transpose8(cat_b, Pn, nwout, lambda: catT[:, :, 0:Pn], ["cat_b"], ["catT"], 0, "nwout")
                for half in range(2):
                    for dc in range(8):
                        pe(lambda e, half=half, dc=dc: e.matmul(P[half][0:Pn, :], lhsT=catT[:, dc, 0:Pn], rhs=wout_b[:, dc, half * 512:(half + 1) * 512],
                                                                start=(dc == 0), stop=(dc == 7)), ["catT", "wout_b"], [f"P{half}"])
                h_res = f"ht{sl}"
                for half in range(2):
                    dv(lambda e, half=half: e.tensor_tensor(out=ht[sl][0:Pn, half * 512:(half + 1) * 512], in0=P[half][0:Pn, :],
                                                            in1=xt[sl][0:Pn, half * 512:(half + 1) * 512], op=ALU.add), [x_res], [f"P{half}", h_res])
                S.dma("sp", lambda e: e.dma_start(out=ytok(tok0, Pn), in_=ht[sl][0:Pn, :]), h_res, reads=[h_res], final=True)
                rms_rstd(ht[sl][0:Pn, :], 1024, Pn, 3, [h_res], junkB, "junkB")
                dv(lambda e: e.tensor_scalar(out=xn2_b[0:Pn, :], in0=ht[sl][0:Pn, :], scalar1=st[0:Pn, 3:4], scalar2=None, op0=ALU.mult),
                   [h_res, "st3"], ["xn2_b"])
                transpose8(xn2_b, Pn, nwffn, lambda: XT[:, :, tok0:tok0 + Pn], ["xn2_b"], [f"XT{tok0}"], 0, "nwffn")

            def adv(g, n=1):
                for _ in range(n):
                    try:
                        next(g)
                    except StopIteration:
                        return
            for _ in front("halo", -1):
                pass
            ntl = min(NT, NTL)
            gens = [front("prompt", ti) for ti in range(ntl)]
            adv(gens[0], 2)
            for ti in range(ntl):
                if ti + 1 < ntl:
                    interleave(lambda ti=ti: adv(gens[ti], 3), lambda ti=ti: adv(gens[ti + 1], 2))
                elif KSAMP:
                    sgen = front("sample", NT)
                    interleave(lambda ti=ti: adv(gens[ti], 3), lambda: adv(sgen, 2))
                else:
                    adv(gens[ti], 3)
            if KSAMP:
                for _ in sgen:
                    pass
            S.emit(final_engine="sp")
        if phases < 2:
            return nc

        with contextlib.ExitStack() as es:
            S = Sched(nc, "b", EXT)
            dv = lambda fn, r=(), w=(): S.op("dve", fn, r, w)
            ac = lambda fn, r=(), w=(): S.op("act", fn, r, w)
            pe = lambda fn, r=(), w=(): S.op("pe", fn, r, w)
            wq_b = sbt(es, "wq_b", [128, 8, 2048], BF16)
            skT_b = sbt(es, "skT_b", [128, 16, 128], BF16)
            iota16 = sbt(es, "iota16", [128, 16], BF16)
            th16 = sbt(es, "th16", [128, 16], F32)
            qpT2 = [sbt(es, f"qpT{i}", [128, 16, 128], BF16) for i in range(2)]
            Ssb2 = [sbt(es, f"Ssb{i}", [128, 16, 128], F32) for i in range(2)]
            wkk = [sbt(es, f"wk{i}", [128, 256], F32) for i in range(2)]
            v16 = sbt(es, "v16", [128, 16, 16], F32)
            i16 = sbt(es, "i16", [128, 16, 16], I32)
            iotai = sbt(es, "iotai", [128, 128], I32)
            mski = sbt(es, "mski", [128, 2], I32)
            i16f = sbt(es, "i16f", [128, 16, 16], BF16)
            cand = sbt(es, "cand", [128, 8, 256], F32)
            comb = sbt(es, "comb", [128, 8, 16, 16], I32)
            best = sbt(es, "best", [128, 8, 16], F32)
            pos = sbt(es, "pos", [128, 8, 16], U32)
            pai = sbt(es, "pai", [128, 8, 16], I32)
            pbi = sbt(es, "pbi", [128, 8, 16], I32)
            paf = sbt(es, "paf", [128, 8, 16], F32)
            pbf = sbt(es, "pbf", [128, 8, 16], BF16)
            eq = sbt(es, "eq", [128, 8, 16, 16], BF16)
            I1f = sbt(es, "I1f", [128, 128], F32)
            I2f = sbt(es, "I2f", [128, 128], F32)
            Gf = sbt(es, "Gf", [128, 128], F32)
            ssum = sbt(es, "ssum", [128, 8], F32)
            Q = [pst(es, f"Q{i}", [128, 512], F32) for i in range(4)]
            Sc = [pst(es, f"Sc{i}", [128, 512], F32) for i in range(4)]
            for g in range(4):
                S.dma("pool", lambda e, g=g: e.dma_start(out=wq_b[:, :, g * 512:(g + 1) * 512], in_=wq_d[:, :, g * 512:(g + 1) * 512]), f"wq_b{g}", writes=[f"wq_b{g}"])
            S.dma("pool", lambda e: e.dma_start(out=skT_b[:, :, :], in_=skT_d), "skT_b", writes=["skT_b"])
            S.dma("sp", lambda e: e.dma_start(out=iotai[:, :], in_=iotai_d), "iotai", writes=["iotai"])
            dv(lambda e: e.memset(mski[:, 0:1], -128), [], ["mski"])
            dv(lambda e: e.memset(mski[:, 1:2], -16384), [], ["mski"])
            if phases >= 3:
                G = 4
                for g in range(NCH // G):
                    S.dma("pool", lambda e, g=g: e.dma_start(out=UTs[g * G:(g + 1) * G], in_=UT_d[g * G:(g + 1) * G]), f"x:UTs{g // 8}")
                    S.dma("pool", lambda e, g=g: e.dma_start(out=Vs[g * G:(g + 1) * G], in_=V_d[g * G:(g + 1) * G]), f"x:Vs{g // 8}")
            dv(lambda e: e.tensor_copy(out=iota16[:, :], in_=iotaf[:, 0:16]), [], ["iota16"])
            dv(lambda e: e.tensor_scalar(out=th16[:, :], in0=iotaf[:, 0:16], scalar1=16.0, scalar2=16.0, op0=ALU.mult, op1=ALU.add), [], ["th16"])

            def route(tok0, Pn):
                par = (tok0 // 128) % 2
                qpT = qpT2[par]
                Ssb = Ssb2[par]
                for hp in range(16):
                    for dc in range(8):
                        pe(lambda e, hp=hp, dc=dc: e.matmul(Q[hp // 4][:, (hp % 4) * 128:(hp % 4) * 128 + Pn], lhsT=wq_b[:, dc, hp * 128:(hp + 1) * 128],
                                                            rhs=XT[:, dc, tok0:tok0 + Pn], start=(dc == 0), stop=(dc == 7)), [f"wq_b{hp // 4}"], [f"Q{hp // 4}"])
                for g in range(4):
                    ac(lambda e, g=g: e.copy(out=qpT[:, g * 4:(g + 1) * 4, 0:Pn], in_=Q[g][:, :].rearrange("p (a t) -> p a t", a=4)[:, :, 0:Pn]), [], [f"Q{g}", f"qpT{par}_{g}"])
                for hp in range(16):
                    pe(lambda e, hp=hp: e.matmul(Sc[hp // 4][0:Pn, (hp % 4) * 128:(hp % 4 + 1) * 128], lhsT=qpT[:, hp, 0:Pn], rhs=skT_b[:, hp, :], start=True, stop=True),
                       [f"qpT{par}_{hp // 4}", "skT_b"], [f"Sc{hp // 4}"])
                for g in range(4):
                    ac(lambda e, g=g: e.copy(out=Ssb[0:Pn, g * 4:(g + 1) * 4, :], in_=Sc[g][0:Pn, :].rearrange("p (a k) -> p a k", a=4)), [], [f"Sc{g}", f"Ssb{par}_{g}"])
                yield 1
                SB = [f"Ssb{par}_{g}" for g in range(4)]
                Si = Ssb[0:Pn].bitcast(I32)
                dv(lambda e: e.scalar_tensor_tensor(out=Si, in0=Si, scalar=mski[0:Pn, 0:1], in1=iotai[0:Pn, :].unsqueeze(1).to_broadcast([Pn, 16, 128]),
                                                    op0=ALU.bitwise_and, op1=ALU.bitwise_or), ["iotai", "mski"], SB)
                for hp0 in range(0, 16, 2):
                    ch = [(hp0, 0), (hp0 + 1, 1)]
                    for (hp, w) in ch:
                        dv(lambda e, hp=hp: e.max(out=v16[0:Pn, hp, 0:8], in_=Ssb[0:Pn, hp, :]), [f"Ssb{par}_{hp // 4}"], [f"v16_{hp}"])
                    for (hp, w) in ch:
                        dv(lambda e, hp=hp, w=w: e.match_replace(out=wkk[w][0:Pn, 0:128], in_to_replace=v16[0:Pn, hp, 0:8], in_values=Ssb[0:Pn, hp, :], imm_value=-1e30),
                           [f"Ssb{par}_{hp // 4}", f"v16_{hp}"], [f"wk{w}"])
                    for (hp, w) in ch:
                        dv(lambda e, hp=hp, w=w: e.max(out=v16[0:Pn, hp, 8:16], in_=wkk[w][0:Pn, 0:128]), [f"wk{w}"], [f"v16b_{hp}"])
                ALLV = [f"v16_{hp}" for hp in range(16)] + [f"v16b_{hp}" for hp in range(16)]
                dv(lambda e: e.tensor_single_scalar(out=i16[0:Pn], in_=v16[0:Pn].bitcast(I32), scalar=127, op=ALU.bitwise_and), ALLV, ["i16all"])
                dv(lambda e: e.tensor_copy(out=i16f[0:Pn], in_=i16[0:Pn]), ["i16all"], ["i16f"])
                v16v = v16[0:Pn].rearrange("p (h s) k -> p h s k", s=2)
                dv(lambda e: e.tensor_tensor(out=cand[0:Pn].rearrange("p h (a b) -> p h a b", b=16), in0=v16v[:, :, 0, :].unsqueeze(3).to_broadcast([Pn, 8, 16, 16]),
                                             in1=v16v[:, :, 1, :].unsqueeze(2).to_broadcast([Pn, 8, 16, 16]), op=ALU.add), ALLV, ["cand"])
                i16v = i16f[0:Pn].rearrange("p (h s) k -> p h s k", s=2)
                dv(lambda e: e.tensor_scalar(out=paf[0:Pn], in0=i16v[:, :, 0, :], scalar1=128.0, scalar2=None, op0=ALU.mult), ["i16f"], ["paf"])
                dv(lambda e: e.tensor_tensor(out=comb[0:Pn], in0=paf[0:Pn].unsqueeze(3).to_broadcast([Pn, 8, 16, 16]),
                                             in1=i16v[:, :, 1, :].unsqueeze(2).to_broadcast([Pn, 8, 16, 16]), op=ALU.add), ["i16f", "paf"], ["comb"])
                Ci = cand[0:Pn].bitcast(I32)
                dv(lambda e: e.scalar_tensor_tensor(out=Ci, in0=Ci, scalar=mski[0:Pn, 1:2], in1=comb[0:Pn].rearrange("p h a b -> p h (a b)"),
                                                    op0=ALU.bitwise_and, op1=ALU.bitwise_or), ["comb", "mski"], ["cand"])
                for h0 in range(0, 8, 2):
                    ch = [(h0, 0), (h0 + 1, 1)]
                    for (h, w) in ch:
                        dv(lambda e, h=h: e.max(out=best[0:Pn, h, 0:8], in_=cand[0:Pn, h, :]), ["cand"], [f"best_{h}"])
                    for (h, w) in ch:
                        dv(lambda e, h=h, w=w: e.match_replace(out=wkk[w][0:Pn, :], in_to_replace=best[0:Pn, h, 0:8], in_values=cand[0:Pn, h, :], imm_value=-1e30),
                           ["cand", f"best_{h}"], [f"wk{w}"])
                    for (h, w) in ch:
                        dv(lambda e, h=h, w=w: e.max(out=best[0:Pn, h, 8:16], in_=wkk[w][0:Pn, :]), [f"wk{w}"], [f"bestb_{h}"])
                ALLB = [f"best_{h}" for h in range(8)] + [f"bestb_{h}" for h in range(8)]
                Bi = best[0:Pn].bitcast(I32)
                dv(lambda e: e.tensor_single_scalar(out=pai[0:Pn], in_=Bi, scalar=16383, op=ALU.bitwise_and), ALLB, ["pai"])
                dv(lambda e: e.tensor_single_scalar(out=pbi[0:Pn], in_=pai[0:Pn], scalar=127, op=ALU.bitwise_and), ["pai"], ["pbi"])
                dv(lambda e: e.tensor_single_scalar(out=pai[0:Pn], in_=pai[0:Pn], scalar=7, op=ALU.logical_shift_right), ["pbi"], ["pai"])
                dv(lambda e: e.tensor_copy(out=I1f[0:Pn, :].rearrange("p (h k) -> p h k", k=16), in_=pai[0:Pn]), ["pai"], ["If0"])
                dv(lambda e: e.tensor_copy(out=I2f[0:Pn, :].rearrange("p (h k) -> p h k", k=16), in_=pbi[0:Pn]), ["pbi"], ["If1"])
                dv(lambda e: e.tensor_single_scalar(out=Bi, in_=Bi, scalar=-16384, op=ALU.bitwise_and), ["pai"], ALLB)
                dv(lambda e: e.tensor_tensor(out=Gf[0:Pn, :].rearrange("p (h k) -> p h k", k=16), in0=best[0:Pn], in1=best[0:Pn, :, 0:1].to_broadcast([Pn, 8, 16]),
                                             op=ALU.subtract), ALLB, ["Gf"])
                ac(lambda e: e.activation(out=Gf[0:Pn, :], in_=Gf[0:Pn, :], func=AF.Exp), ["Gf"], ["Gf"])
                dv(lambda e: e.tensor_reduce(out=ssum[0:Pn, :], in_=Gf[0:Pn, :].rearrange("p (h k) -> p h k", k=16), axis=AX.X, op=ALU.add), ["Gf"], ["ssum"])
                dv(lambda e: e.reciprocal(out=ssum[0:Pn, :], in_=ssum[0:Pn, :]), ["ssum"], ["ssum"])
                dv(lambda e: e.tensor_tensor(out=Gf[0:Pn, :].rearrange("p (h k) -> p h k", k=16), in0=Gf[0:Pn, :].rearrange("p (h k) -> p h k", k=16),
                                             in1=ssum[0:Pn, :].unsqueeze(2).to_broadcast([Pn, 8, 16]), op=ALU.mult), ["ssum"], ["Gf"])
                for i, (src, dstT, rn) in enumerate(((I1f, I1T, "If0"), (I2f, I2T, "If1"), (Gf, GT, "Gf"))):
                    pe(lambda e, i=i, src=src: e.transpose(out=Q[i][:, 0:Pn], in_=src[0:Pn, :], identity=identf[0:Pn, 0:Pn]), [rn], [f"Q{i}"])
                    ac(lambda e, i=i, dstT=dstT: e.copy(out=dstT[:, tok0:tok0 + Pn], in_=Q[i][:, 0:Pn]), [], [f"Q{i}", f"T{i}_{tok0}"])

            rgens = [route(ti * 128, 128) for ti in range(NT)] + [route(TPC, NS)]
            next(rgens[0])
            for i in range(len(rgens)):
                if i + 1 < len(rgens):
                    next(rgens[i + 1])
                for _ in rgens[i]:
                    pass
            S.emit(final_engine="sp")
        if phases < 3:
            return nc

        with contextlib.ExitStack() as es:
            S = Sched(nc, "c", EXT)
            dv = lambda fn, r=(), w=(): S.op("dve", fn, r, w)
            ac = lambda fn, r=(), w=(): S.op("act", fn, r, w)
            pe = lambda fn, r=(), w=(): S.op("pe", fn, r, w)
            pl = lambda fn, r=(), w=(): S.op("pool", fn, r, w)
            NR = 4
            WT = sbt(es, "WT", [128, NCH, TT], BF16)
            UTb = [sbt(es, f"UTb{i}", [128, 1024], BF16) for i in range(NR)]
            Vb = [sbt(es, f"Vb{i}", [128, 1024], BF16) for i in range(NR)]
            TG = 16
            NOS = 3
            Ab = [sbt(es, f"Ab{i}", [128, TG, 128], BF16) for i in range(NOS)]
            Bb = [sbt(es, f"Bb{i}", [128, TG, 128], BF16) for i in range(NOS)]
            Gs = [sbt(es, f"Gs{i}", [128, TT], BF16) for i in range(2)]
            HT = [sbt(es, f"HT{i}", [128, TT], BF16) for i in range(2)]
            hb = [sbt(es, f"hb{i}", [128, 1024], F32) for i in range(2)]
            Oa = [pst(es, f"Oa{i}", [128, 512], F32) for i in range(6)]
            A = [pst(es, f"A{i}", [128, 512], F32) for i in range(2)]
            tiles = [(t0, min(TT, NTOK - t0)) for t0 in range(0, NTOK, TT)]
            ci = 0
            oh = 0
            for (t0, tn) in tiles:
                subs = [(s0, min(128, tn - s0)) for s0 in range(0, tn, 128)]
                for tg in range(0, tn, TG):
                    ng = min(TG, tn - tg)
                    sl = oh % NOS
                    oh += 1
                    tb = t0 + tg
                    for k in range(ng):
                        dv(lambda e, sl=sl, t=tb + k, k=k: e.tensor_scalar(out=Bb[sl][:, k, :], in0=iotab[:, :], scalar1=I2T[:, t:t + 1], scalar2=None,
                                                                          op0=ALU.is_equal), [], [f"Bb{sl}_{k // 4}"])
                        dv(lambda e, sl=sl, t=tb + k, k=k: e.tensor_scalar(out=Ab[sl][:, k, :], in0=iotab[:, :], scalar1=I1T[:, t:t + 1], scalar2=GT[:, t:t + 1],
                                                                          op0=ALU.is_equal, op1=ALU.mult), [], [f"Ab{sl}_{k // 4}"])
                    for tq in range(0, ng, 4):
                        nq = min(4, ng - tq)
                        bank = ((tg + tq) // 4) % 2
                        for k in range(nq):
                            pe(lambda e, sl=sl, k=k, tq=tq, bank=bank: e.matmul(A[bank][:, k * 128:(k + 1) * 128], lhsT=Bb[sl][:, tq + k, :], rhs=Ab[sl][:, tq + k, :],
                                                                             start=True, stop=True), [f"Ab{sl}_{tq // 4}", f"Bb{sl}_{tq // 4}"], [f"A{bank}"])
                        ac(lambda e, bank=bank, tq=tq, nq=nq, tg=tg: e.copy(out=WT[:, :, tg + tq:tg + tq + nq],
                                                                          in_=A[bank][:, 0:nq * 128].rearrange("p (t i) -> p i t", t=nq)),
                           [], [f"A{bank}", "WT"])
                def mm2(c, sl, ab):
                    for si, (s0, sn) in enumerate(subs):
                        for half in range(2):
                            pe(lambda e, si=si, s0=s0, sn=sn, half=half, ab=ab, sl=sl, c=c: e.matmul(Oa[si * 2 + half][0:sn, :], lhsT=HT[ab][:, s0:s0 + sn],
                                                                                                  rhs=Vb[sl][:, half * 512:(half + 1) * 512],
                                                                                                  start=(c == 0), stop=(c == NCH - 1)),
                               [f"HT{ab}", f"Vb{sl}"], [f"Oa{si * 2 + half}"])
                prev = None
                for c in range(NCH):
                    sl = ci % NR
                    ab = ci % 2
                    ci += 1
                    xq = c // 32
                    S.dma("sp", lambda e, c=c, sl=sl: e.dma_start(out=UTb[sl][:, :], in_=UTs[c]), f"UTb{sl}", writes=[f"UTb{sl}"],
                          extra_waits=([(f"x:UTs{xq}", 128)] if t0 == 0 else []))
                    S.dma("sp", lambda e, c=c, sl=sl: e.dma_start(out=Vb[sl][:, :], in_=Vs[c]), f"Vb{sl}", writes=[f"Vb{sl}"],
                          extra_waits=([(f"x:Vs{xq}", 128)] if t0 == 0 else []))
                    for dc in range(8):
                        pe(lambda e, dc=dc, sl=sl, ab=ab, t0=t0, tn=tn: e.matmul(A[ab][:, 0:tn], lhsT=UTb[sl][:, dc * 128:(dc + 1) * 128], rhs=XT[:, dc, t0:t0 + tn],
                                                                   start=(dc == 0), stop=(dc == 7)), [f"UTb{sl}"], [f"A{ab}"])
                    ac(lambda e, ab=ab, tn=tn: e.activation(out=Gs[ab][:, 0:tn], in_=A[ab][:, 0:tn], func=AF.Gelu), [], [f"A{ab}", f"Gs{ab}"])
                    dv(lambda e, ab=ab, c=c, tn=tn: e.tensor_tensor(out=HT[ab][:, 0:tn], in0=Gs[ab][:, 0:tn], in1=WT[:, c, 0:tn], op=ALU.mult), [f"Gs{ab}", "WT"], [f"HT{ab}"])
                    if prev is not None:
                        mm2(*prev)
                    prev = (c, sl, ab)
                mm2(*prev)
                for si, (s0, sn) in enumerate(subs):
                    hs = si % 2
                    S.dma("sp", lambda e, hs=hs, s0=s0, sn=sn, t0=t0: e.dma_start(out=hb[hs][0:sn, :], in_=ytok(t0 + s0, sn)), f"hb{hs}", writes=[f"hb{hs}"])
                    for half in range(2):
                        dv(lambda e, hs=hs, sn=sn, si=si, half=half: e.tensor_tensor(out=hb[hs][0:sn, half * 512:(half + 1) * 512], in0=Oa[si * 2 + half][0:sn, :],
                                                                                    in1=hb[hs][0:sn, half * 512:(half + 1) * 512], op=ALU.add),
                           [], [f"Oa{si * 2 + half}", f"hb{hs}"])
                    S.dma("sp", lambda e, hs=hs, s0=s0, sn=sn, t0=t0: e.dma_start(out=ytok(t0 + s0, sn), in_=hb[hs][0:sn, :]), f"hb{hs}", reads=[f"hb{hs}"], final=True)
            S.emit(final_engine="sp")
    return nc


def _host_inputs(x_prompt, x_sample, cache_k, cache_v, norm_mix_w, w_in, q_norm_w, k_norm_w, sinks,
                 gm_v_norm_w, w_spatial, b_spatial, out_norm_w, w_out, norm_ffn_w, w_query, sub_keys,
                 expert_u, expert_v):
    f = np.float32
    rep = lambda v: np.ascontiguousarray(np.broadcast_to(np.asarray(v, f).reshape(1, -1), (128, np.asarray(v).size)))
    col8 = lambda v: np.ascontiguousarray(np.asarray(v, f).reshape(8, 128).T)
    half = 32
    inv = (np.float32(10000.0) ** (-(np.arange(half, dtype=f) / f(half)))).astype(f)

    def rope_tab(pos):
        ang = pos.astype(f)[:, None] * inv[None, :]
        c, s_ = np.cos(ang).astype(f), np.sin(ang).astype(f)
        return np.ascontiguousarray(np.concatenate([c, c, s_], axis=1))

    jj = np.arange(128)[:, None]
    ii = np.arange(128)[None, :]
    m_prev = (jj > ii).astype(f)
    m_own = (jj <= ii).astype(f)
    common = dict(
        ident=np.eye(128, dtype=f), iotai=np.ascontiguousarray(np.broadcast_to(np.arange(128, dtype=np.int32), (128, 128))), iota=np.ascontiguousarray(np.broadcast_to(np.arange(128, dtype=f), (128, 128))),
        tril=(jj <= ii).astype(f),
        nwmix=col8(norm_mix_w[0]), nwout=col8(out_norm_w[0]), nwffn=col8(norm_ffn_w[0]),
        nwqk=rep(np.concatenate([np.tile(q_norm_w[0], 8), np.tile(k_norm_w[0], 2)])),
        nwgv=rep(gm_v_norm_w[0].reshape(-1)), sinksr=rep(sinks[0]),
        ws0=rep(w_spatial[0, :, 0, 0]), b0=rep(b_spatial[0, :, 0]), bT=np.ascontiguousarray(b_spatial[0].T.astype(f)),
        win=np.ascontiguousarray(w_in[0].reshape(8, 128, 1792).transpose(1, 0, 2)),
        wout=np.ascontiguousarray(w_out[0].reshape(8, 128, 1024).transpose(1, 0, 2)),
        wq=np.ascontiguousarray(w_query[0].reshape(8, 128, 2048).transpose(1, 0, 2)),
        skT=np.ascontiguousarray(sub_keys[0].reshape(16, 128, 128).transpose(2, 0, 1)),
        wsT=np.ascontiguousarray(w_spatial[0].transpose(2, 0, 1)),
        UT=np.ascontiguousarray(expert_u[0].reshape(NCH, 128, 8, 128).transpose(0, 3, 2, 1).reshape(NCH, 128, 1024)),
        Vv=np.ascontiguousarray(expert_v[0].reshape(NCH, 128, 1024)),
    )
    maps = []
    xp_all = x_prompt[0]
    for c in range(NCORES):
        lo = c * TPC
        if c == 0:
            xpc = np.concatenate([np.zeros((128, 1024), f), xp_all[0:TPC]], axis=0)
        else:
            xpc = xp_all[lo - 128:lo + TPC]
        pos = np.maximum(np.arange(lo - 128, lo + TPC), 0)
        mk = np.stack([m_prev if c > 0 else np.zeros_like(m_prev), m_prev, m_own])
        d = dict(common)
        d.update(
            xp=np.ascontiguousarray(xpc), xs=np.ascontiguousarray(x_sample[c * NS:(c + 1) * NS, 0, :]),
            ck=np.ascontiguousarray(cache_k[0, c * NS:(c + 1) * NS].reshape(NS, 128, 128)),
            cv=np.ascontiguousarray(cache_v[0, c * NS:(c + 1) * NS].reshape(NS, 128, 128)),
            rtp=rope_tab(pos), rts=rope_tab(np.full((NS,), 16384)), masks=np.ascontiguousarray(mk),
        )
        maps.append(d)
    return maps


def _assemble(results):
    f = np.float32
    y_p = np.concatenate([r["y_p"] for r in results], axis=0).reshape(1, NCORES * TPC, 1024)
    y_s = np.concatenate([r["y_s"] for r in results], axis=0).reshape(NCORES * NS, 1, 1024)
    nk_p = results[-1]["nk_p"].reshape(1, 1, 128, 2, 64)
    nv_p = results[-1]["nv_p"].reshape(1, 1, 128, 2, 64)
    nk_s = np.concatenate([r["nk_s"] for r in results], axis=0).reshape(1, NCORES * NS, 128, 2, 64)
    nv_s = np.concatenate([r["nv_s"] for r in results], axis=0).reshape(1, NCORES * NS, 128, 2, 64)
    ngv = np.concatenate([r["ngv_s"] for r in results], axis=0).reshape(1, NCORES * NS, 1, 8, 64)
    return tuple(np.ascontiguousarray(a.astype(f)) for a in (y_p, y_s, nk_p, nv_p, nk_s, nv_s, ngv))


def kernel(**inputs):
    inputs = {k: np.asarray(v) for k, v in inputs.items()}
    maps = _host_inputs(**inputs)
    nc = build_nc()
    res = run_bass_kernel_spmd(nc, maps, core_ids=list(range(NCORES)))
    return _assemble(res.results)
```
